# Optimizing a Trainium2 kernel written in Bass

```python
import jax, jax.numpy as jnp
from jax import lax
import numpy as np

D_MODEL = 2048
BATCH = 2
SEQ = 4096
DEPTH = 1

GRID_W = 64
CTX_LEN = 256
MIX_WIDTH = D_MODEL
HEAD_DIM = 128
ATTN_WIDTH = MIX_WIDTH // 2
GMLP_WIDTH = MIX_WIDTH - ATTN_WIDTH
N_Q_HEADS = ATTN_WIDTH // HEAD_DIM
N_KV_HEADS = 2
KV_WIDTH = N_KV_HEADS * HEAD_DIM
GMLP_HEADS = 8
GMLP_HEAD_DIM = GMLP_WIDTH // GMLP_HEADS
CHUNK = 128
WINDOW = 128
BLOCK = 128
D_FF = 4 * D_MODEL
N_MOD = 6
Q_END = ATTN_WIDTH
K_END = Q_END + KV_WIDTH
V_END = K_END + KV_WIDTH
U_END = V_END + GMLP_WIDTH
IN_WIDTH = U_END + GMLP_WIDTH
ROPE_BASE = 10000.0
LN_EPS = 1e-5
NEG_INF = -1e30
ALPHA = (2 * DEPTH) ** 0.25
BETA = (8 * DEPTH) ** -0.25

kernel_name = "hymba_style_window_gqa_gmlp_dit_block"


def layer_norm(x, g, b):
    xf = x.astype(jnp.float32)
    mu = jnp.mean(xf, axis=-1, keepdims=True)
    var = jnp.mean(jnp.square(xf - mu), axis=-1, keepdims=True)
    y = (xf - mu) * lax.rsqrt(var + LN_EPS)
    return (y * g.astype(jnp.float32) + b.astype(jnp.float32)).astype(x.dtype)


def modulate(x, shift, scale):
    return x * (1 + scale) + shift


def adaln(cond, w_ada, b_ada):
    m = jax.nn.silu(cond) @ w_ada + b_ada
    return m.reshape(cond.shape[0], N_MOD, D_MODEL)


def axial_rope_tables(n_tokens):
    rows = n_tokens // GRID_W
    row = jnp.repeat(jnp.arange(rows, dtype=jnp.float32), GRID_W)
    col = jnp.tile(jnp.arange(GRID_W, dtype=jnp.float32), rows)
    n_freq = HEAD_DIM // 4
    inv_freq = ROPE_BASE ** (-jnp.arange(n_freq, dtype=jnp.float32) / n_freq)
    ang_r = row[:, None] * inv_freq[None, :]
    ang_c = col[:, None] * inv_freq[None, :]
    return (jnp.cos(ang_r), jnp.sin(ang_r), jnp.cos(ang_c), jnp.sin(ang_c))


def rotate_half_rope(x, cos, sin):
    x1, x2 = jnp.split(x, 2, axis=-1)
    cos = cos[None, :, None, :]
    sin = sin[None, :, None, :]
    return jnp.concatenate([x1 * cos - x2 * sin, x1 * sin + x2 * cos], axis=-1)


def axial_rope(x, tables):
    cos_r, sin_r, cos_c, sin_c = tables
    xr, xc = jnp.split(x, 2, axis=-1)
    y = jnp.concatenate([rotate_half_rope(xr, cos_r, sin_r),
                         rotate_half_rope(xc, cos_c, sin_c)], axis=-1)
    return y.astype(x.dtype)


def split_projection(proj):
    b, s, _ = proj.shape
    q, k, v, u, g = jnp.split(proj, [Q_END, K_END, V_END, U_END], axis=-1)
    q = q.reshape(b, s, N_Q_HEADS, HEAD_DIM)
    k = k.reshape(b, s, N_KV_HEADS, HEAD_DIM)
    v = v.reshape(b, s, N_KV_HEADS, HEAD_DIM)
    return q, k, v, u, g


def windowed_attention(q, k, v, k_ctx, v_ctx, sink):
    b, s, hq, d = q.shape
    nb = s // BLOCK
    grp = hq // N_KV_HEADS
    scale = d ** -0.5
    qb = q.reshape(b, nb, BLOCK, N_KV_HEADS, grp, d)
    pad = ((0, 0), (BLOCK, BLOCK), (0, 0), (0, 0))
    kp = jnp.pad(k, pad).reshape(b, nb + 2, BLOCK, N_KV_HEADS, d)
    vp = jnp.pad(v, pad).reshape(b, nb + 2, BLOCK, N_KV_HEADS, d)
    kb = jnp.concatenate([kp[:, :-2], kp[:, 1:-1], kp[:, 2:]], axis=2)
    vb = jnp.concatenate([vp[:, :-2], vp[:, 1:-1], vp[:, 2:]], axis=2)
    s_loc = jnp.einsum('bnqhgd,bnkhd->bnhgqk', qb, kb,
                       preferred_element_type=jnp.float32) * scale
    q_idx = jnp.arange(nb)[:, None, None] * BLOCK + jnp.arange(BLOCK)[None, :, None]
    k_idx = jnp.arange(nb)[:, None, None] * BLOCK - BLOCK + jnp.arange(3 * BLOCK)[None, None, :]
    band = (jnp.abs(k_idx - q_idx) <= WINDOW) & (k_idx >= 0) & (k_idx < s)
    s_loc = jnp.where(band[None, :, None, None], s_loc, NEG_INF)
    s_ctx = jnp.einsum('bnqhgd,bchd->bnhgqc', qb, k_ctx,
                       preferred_element_type=jnp.float32) * scale
    s_sink = jnp.broadcast_to(
        sink.astype(jnp.float32).reshape(1, 1, N_KV_HEADS, grp, 1, 1),
        s_loc.shape[:-1] + (1,))
    p = jax.nn.softmax(jnp.concatenate([s_loc, s_ctx, s_sink], axis=-1), axis=-1)
    n_loc = 3 * BLOCK
    n_ctx = k_ctx.shape[1]
    p_loc = p[..., :n_loc].astype(v.dtype)
    p_ctx = p[..., n_loc:n_loc + n_ctx].astype(v.dtype)
    o = (jnp.einsum('bnhgqk,bnkhd->bnqhgd', p_loc, vb)
         + jnp.einsum('bnhgqc,bchd->bnqhgd', p_ctx, v_ctx))
    return o.reshape(b, s, hq * d)


def context_attention(q, k, v, sink):
    b, n, hq, d = q.shape
    grp = hq // N_KV_HEADS
    qg = q.reshape(b, n, N_KV_HEADS, grp, d)
    sc = jnp.einsum('bqhgd,bkhd->bhgqk', qg, k,
                    preferred_element_type=jnp.float32) * (d ** -0.5)
    s_sink = jnp.broadcast_to(
        sink.astype(jnp.float32).reshape(1, N_KV_HEADS, grp, 1, 1), sc.shape[:-1] + (1,))
    p = jax.nn.softmax(jnp.concatenate([sc, s_sink], axis=-1), axis=-1)
    o = jnp.einsum('bhgqk,bkhd->bqhgd', p[..., :n].astype(v.dtype), v)
    return o.reshape(b, n, hq * d)


def chunk_spatial_gate(u, g, ln_g, ln_b, w_s, b_s):
    b, s, _ = g.shape
    u = jax.nn.gelu(u)
    g = layer_norm(jax.nn.gelu(g), ln_g, ln_b)
    gc = g.reshape(b, s // CHUNK, CHUNK, GMLP_HEADS, GMLP_HEAD_DIM)
    mixed = jnp.einsum('hpq,bnqhd->bnphd', w_s, gc) + b_s.T[None, None, :, :, None]
    return u * mixed.reshape(b, s, GMLP_WIDTH)


def squared_relu_mlp(h, w1, w2):
    return jnp.square(jax.nn.relu(h @ w1)) @ w2


def context_kv(h_ctx, w_in):
    b, n, _ = h_ctx.shape
    kv = h_ctx @ w_in[:, Q_END:V_END]
    k, v = jnp.split(kv, 2, axis=-1)
    return (k.reshape(b, n, N_KV_HEADS, HEAD_DIM), v.reshape(b, n, N_KV_HEADS, HEAD_DIM))


def context_mixer(h_ctx, w_in, sink, gln_g, gln_b, w_s, b_s, w_out):
    q, k, v, u, g = split_projection(h_ctx @ w_in)
    attn = context_attention(q, k, v, sink)
    gm = chunk_spatial_gate(u, g, gln_g, gln_b, w_s, b_s)
    return jnp.concatenate([attn, gm], axis=-1) @ w_out, k, v


def latent_mixer(h, k_ctx, v_ctx, rope, w_in, sink, gln_g, gln_b, w_s, b_s, w_out):
    q, k, v, u, g = split_projection(h @ w_in)
    q = axial_rope(q, rope)
    k = axial_rope(k, rope)
    attn = windowed_attention(q, k, v, k_ctx, v_ctx, sink)
    gm = chunk_spatial_gate(u, g, gln_g, gln_b, w_s, b_s)
    return jnp.concatenate([attn, gm], axis=-1) @ w_out


def setup_inputs(seed: int = 0) -> dict:
    key = jax.random.key(seed)
    ks = jax.random.split(key, 20)
    f32 = jnp.float32
    nrm = lambda k, shape, s: jax.random.normal(k, shape, f32) * s
    return {
        "x": nrm(ks[0], (BATCH, SEQ, D_MODEL), 1.0),
        "c": nrm(ks[1], (BATCH, D_MODEL), 1.0),
        "ctx": nrm(ks[2], (BATCH, CTX_LEN, D_MODEL), 1.0),
        "c_ctx": nrm(ks[3], (D_MODEL,), 1.0),
        "w_ada": nrm(ks[4], (DEPTH, D_MODEL, N_MOD * D_MODEL), 0.5 * D_MODEL ** -0.5),
        "b_ada": nrm(ks[5], (DEPTH, N_MOD * D_MODEL), 0.02),
        "w_in": nrm(ks[6], (DEPTH, D_MODEL, IN_WIDTH), D_MODEL ** -0.5),
        "attn_sink": nrm(ks[7], (DEPTH, N_Q_HEADS), 1.0),
        "gmlp_ln_g": 1.0 + nrm(ks[8], (DEPTH, GMLP_WIDTH), 0.02),
        "gmlp_ln_b": nrm(ks[9], (DEPTH, GMLP_WIDTH), 0.02),
        "gmlp_w_s": nrm(ks[10], (DEPTH, GMLP_HEADS, CHUNK, CHUNK), CHUNK ** -0.5),
        "gmlp_b_s": 1.0 + nrm(ks[11], (DEPTH, GMLP_HEADS, CHUNK), 0.02),
        "w_out": nrm(ks[12], (DEPTH, MIX_WIDTH, D_MODEL), BETA * MIX_WIDTH ** -0.5),
        "ln1_g": 1.0 + nrm(ks[13], (DEPTH, D_MODEL), 0.02),
        "ln1_b": nrm(ks[14], (DEPTH, D_MODEL), 0.02),
        "w_ff1": nrm(ks[15], (DEPTH, D_MODEL, D_FF), D_MODEL ** -0.5),
        "w_ff2": nrm(ks[16], (DEPTH, D_FF, D_MODEL), BETA * D_FF ** -0.5),
        "ln2_g": 1.0 + nrm(ks[17], (DEPTH, D_MODEL), 0.02),
        "ln2_b": nrm(ks[18], (DEPTH, D_MODEL), 0.02),
    }


def reference(x, c, ctx, c_ctx, w_ada, b_ada, w_in, attn_sink, gmlp_ln_g, gmlp_ln_b,
              gmlp_w_s, gmlp_b_s, w_out, ln1_g, ln1_b, w_ff1, w_ff2, ln2_g, ln2_b):
    rope = axial_rope_tables(x.shape[1])
    for l in range(DEPTH):
        mod = adaln(c, w_ada[l], b_ada[l])[:, :, None, :]
        mod_c = adaln(c_ctx[None, :], w_ada[l], b_ada[l])[:, :, None, :]
        mixer_params = (w_in[l], attn_sink[l], gmlp_ln_g[l], gmlp_ln_b[l],
                        gmlp_w_s[l], gmlp_b_s[l], w_out[l])
        h_ctx = modulate(ctx, mod_c[:, 0], mod_c[:, 1])
        if l < DEPTH - 1:
            mix_ctx, k_ctx, v_ctx = context_mixer(h_ctx, *mixer_params)
            ctx_new = layer_norm(ALPHA * ctx + mod_c[:, 2] * mix_ctx, ln1_g[l], ln1_b[l])
            ff_ctx = squared_relu_mlp(modulate(ctx_new, mod_c[:, 3], mod_c[:, 4]),
                                      w_ff1[l], w_ff2[l])
            ctx_next = layer_norm(ALPHA * ctx_new + mod_c[:, 5] * ff_ctx, ln2_g[l], ln2_b[l])
        else:
            k_ctx, v_ctx = context_kv(h_ctx, w_in[l])
            ctx_next = ctx
        h = modulate(x, mod[:, 0], mod[:, 1])
        mix = latent_mixer(h, k_ctx, v_ctx, rope, *mixer_params)
        x = layer_norm(ALPHA * x + mod[:, 2] * mix, ln1_g[l], ln1_b[l])
        ff = squared_relu_mlp(modulate(x, mod[:, 3], mod[:, 4]), w_ff1[l], w_ff2[l])
        x = layer_norm(ALPHA * x + mod[:, 5] * ff, ln2_g[l], ln2_b[l])
        ctx = ctx_next
    return x
```

```python
import contextlib
import numpy as np
import concourse.bass as bass
import concourse.mybir as mybir
from concourse.bass_utils import run_bass_kernel_spmd

F32 = mybir.dt.float32
BF16 = mybir.dt.bfloat16
AF = mybir.ActivationFunctionType
ALU = mybir.AluOpType

NCORES = 8
D = 2048
SEQ = 4096
TOK = 1024
NT = 8
NH = 10
CTX = 256
DFF = 8192
ALPHA = float(2.0 ** 0.25)
EPS = 1e-5
QSCALE = float(128.0 ** -0.5)
DEBUG = False

ENGS = ["sync", "gpsimd", "tensor", "scalar", "vector"]


class Buf:
    __slots__ = ("name", "region", "lo", "hi", "ap", "last_w", "last_r")

    def __init__(self, name, region, lo, hi, ap):
        self.name, self.region, self.lo, self.hi, self.ap = name, region, lo, hi, ap
        self.last_w = None
        self.last_r = {}


class Op:
    __slots__ = ("eng", "fn", "deps", "needed", "val", "dma_sem", "dma_mode", "idx")

    def __init__(self, eng, fn, dma_sem=None, dma_mode=None):
        self.eng, self.fn = eng, fn
        self.deps = []
        self.needed = False
        self.val = None
        self.dma_sem, self.dma_mode = dma_sem, dma_mode


class Plan:
    def __init__(self):
        self.ops = {e: [] for e in ENGS}
        self.regions = {}
        self.dma_cnt = {}
        self.dma_sems = []
        self.bank_i = 0
        self.nops = 0

    def buf(self, name, region, lo, nbytes, ap):
        b = Buf(name, region, lo, lo + nbytes, ap)
        self.regions.setdefault(region, []).append(b)
        return b

    def _key(self, op):
        return ("dma", op.dma_sem) if op.dma_sem else op.eng

    def add(self, eng, fn, reads=(), writes=(), dma_sem=None, dma_mode=None):
        op = Op(eng, fn, dma_sem, dma_mode)
        op.idx = self.nops
        self.nops += 1
        deps = {}
        for b in reads:
            if b.last_w is not None:
                deps[id(b.last_w)] = b.last_w
            if b.region == "PS":
                for r in b.last_r.values():
                    if r.eng != eng:
                        deps[id(r)] = r
        for b in writes:
            for o in self.regions[b.region]:
                if o.hi <= b.lo or o.lo >= b.hi:
                    continue
                if o.last_w is not None:
                    deps[id(o.last_w)] = o.last_w
                for r in o.last_r.values():
                    deps[id(r)] = r
                if o is not b:
                    o.last_w = None
                    o.last_r = {}
        for d in deps.values():
            if d is op:
                continue
            if d.dma_sem is None and d.eng == eng and dma_sem is None and eng == "tensor":
                continue
            op.deps.append(d)
            d.needed = True
        for b in writes:
            b.last_w = op
            b.last_r = {}
        for b in reads:
            b.last_r[self._key(op)] = op
        if dma_sem is not None:
            if dma_sem not in self.dma_cnt:
                self.dma_cnt[dma_sem] = 0
                self.dma_sems.append(dma_sem)
            self.dma_cnt[dma_sem] += 16
            op.val = self.dma_cnt[dma_sem]
        self.ops[eng].append(op)
        return op

    def region_last_use(self, region):
        m = -1
        for b in self.regions.get(region, []):
            if b.last_w is not None:
                m = max(m, b.last_w.idx)
            for r in b.last_r.values():
                m = max(m, r.idx)
        return m

    def finalize(self):
        for e in ENGS:
            c = 0
            for op in self.ops[e]:
                if op.dma_sem is None and op.needed:
                    c += 1
                    op.val = c
        for e in ENGS:
            for op in self.ops[e]:
                if op.dma_sem is not None and op.dma_mode == "group":
                    op.val = self.dma_cnt[op.dma_sem]

    def emit(self, eng_name, eng, sems):
        waited = {}
        for op in self.ops[eng_name]:
            for d in op.deps:
                sname = ("dma:" + d.dma_sem) if d.dma_sem else ("eng:" + d.eng)
                if waited.get(sname, 0) >= d.val:
                    continue
                eng.wait_ge(sems[sname], d.val)
                waited[sname] = d.val
            ins = op.fn(eng)
            if op.dma_sem is not None:
                ins.then_inc(sems["dma:" + op.dma_sem], 16)
            elif op.needed:
                ins.then_inc(sems["eng:" + eng_name], 1)


def build_program():
    nc = bass.Bass("TRN2", target_bir_lowering=False)
    P = Plan()

    def dram_in(name, shape):
        return nc.dram_tensor(name, list(shape), F32, kind="ExternalInput").ap()

    x_tm = dram_in("x_tm", [TOK, D])
    xT = dram_in("xT", [D, NH * 128])
    ctxT = dram_in("ctxT", [D, CTX])
    cvec = dram_in("cvec", [128, 32])
    w_ada = dram_in("w_ada", [D, 6 * D])
    b_ada = dram_in("b_ada", [1, 6 * D])
    b_adaT = dram_in("b_adaT", [128, 96])
    w_in = dram_in("w_in", [D, 3584])
    sink = dram_in("sink", [1, 8])
    gln_g = dram_in("gln_g", [1, 1024])
    gln_b = dram_in("gln_b", [1, 1024])
    w_sT = dram_in("w_sT", [128, 1024])
    b_s = dram_in("b_s", [1, 1024])
    w_out = dram_in("w_out", [D, D])
    ln1_g = dram_in("ln1_g", [1, D])
    ln1_b = dram_in("ln1_b", [1, D])
    ln1_gT = dram_in("ln1_gT", [128, 16])
    ln1_bT = dram_in("ln1_bT", [128, 16])
    w_ff1 = dram_in("w_ff1", [D, DFF])
    w_ff2 = dram_in("w_ff2", [DFF, D])
    ln2_g = dram_in("ln2_g", [1, D])
    ln2_b = dram_in("ln2_b", [1, D])
    ropeC = dram_in("ropeC", [NH * 128, 128])
    ropeS = dram_in("ropeS", [NH * 128, 128])
    masks = dram_in("masks", [128, 512])
    ident = dram_in("ident", [128, 128])
    sel4 = dram_in("sel4", [128, 2])
    y = nc.dram_tensor("y", [TOK, D], F32, kind="ExternalOutput").ap()
    dbg = {}

    es = contextlib.ExitStack()
    with es:
        def sb(name, nbytes):
            return es.enter_context(nc.sbuf_tensor(name, [128, nbytes // 4], F32))

        RA_ = sb("regA", 65536)
        RB_ = sb("regB", 16384)
        RC_ = sb("regC", 32768)
        RW_ = [sb("regW%d" % i, 16384) for i in range(4)]
        RD_ = sb("regD", 30720)
        PS_ = [es.enter_context(nc.psum_tensor("ps%d" % i, [128, 512], F32)) for i in range(8)]
        tiles = {"A": RA_, "B": RB_, "C": RC_, "D": RD_, "W0": RW_[0], "W1": RW_[1], "W2": RW_[2], "W3": RW_[3]}

        def mk(name, region, off, dt, shape):
            n = int(np.prod(shape))
            nbytes = n * (2 if dt == BF16 else 4)
            assert off % 4 == 0 and nbytes % 4 == 0
            assert off + nbytes <= tiles[region].shape[1] * 4, (name, region, off, nbytes)
            ap = tiles[region][:, off // 4:(off + nbytes) // 4]
            if dt == BF16:
                ap = ap.bitcast(BF16)
            if len(shape) == 2:
                ap = ap.rearrange("p (a b) -> p a b", a=shape[0], b=shape[1])
            elif len(shape) == 3:
                ap = ap.rearrange("p (a b c) -> p a b c", a=shape[0], b=shape[1], c=shape[2])
            return P.buf(name, region, off, nbytes, ap)

        PSB = [P.buf("ps%d" % i, "PS", i * 2048, 2048, PS_[i][:]) for i in range(8)]

        bank_reserved = set()

        def bank():
            while (P.bank_i % 8) in bank_reserved:
                P.bank_i += 1
            b = PSB[P.bank_i % 8]
            P.bank_i += 1
            return b

        HT = [mk("HT%d" % k, "A", k * 2560, BF16, [NH * 128]) for k in range(16)]
        KT = [mk("KT%d" % h, "A", 40960 + h * 2560, BF16, [NH * 128]) for h in range(2)]
        Vb = [mk("V%d" % t, "A", 46080 + t * 512, BF16, [256]) for t in range(NH)]
        KCT = mk("KCT", "A", 51200, BF16, [2, 256])
        VC = mk("VC", "A", 52224, BF16, [2, 256])
        ROPEC = mk("ROPEC", "A", 53248, F32, [NH, 128])
        ROPES = mk("ROPES", "A", 58368, F32, [NH, 128])
        X = [[mk("X%d_%d" % (t, n), "A", t * 8192 + n * 2048, F32, [512]) for n in range(4)] for t in range(NT)]
        Xall = mk("Xall", "A", 0, F32, [NT, 2048])
        P.regions["A"].remove(Xall)

        QT = [mk("QT%d" % h, "B", h * 2048, BF16, [TOK]) for h in range(8)]
        GUT = [mk("GUT%d" % h, "B", h * 2048, BF16, [TOK]) for h in range(8)]
        QTall = mk("QTall", "B", 0, BF16, [8, TOK]); P.regions["B"].remove(QTall)
        GATE1 = [mk("GATE1_%d" % n, "B", n * 2048, F32, [512]) for n in range(4)]
        ZB = [mk("ZB%d" % i, "B", 8192 + i * 4096, BF16, [2048]) for i in range(2)]
        HID = [[mk("HID%d_%d" % (k, g), "B", g * 8192 + k * 1024, BF16, [512]) for g in range(2)] for k in range(8)]

        XTR = [mk("XTR%d" % i, "C", i * 5120, F32, [NH * 128]) for i in range(3)]
        HCT = [mk("HCT%d" % k, "C", 15360 + k * 512, BF16, [CTX]) for k in range(16)]
        LB = mk("LB", "C", 23552, BF16, [16, 32])
        CTXT = mk("CTXT", "B", 0, F32, [16, CTX])
        CAT = [[mk("CAT%d_%d" % (k, t), "C", k * 2048 + t * 256, BF16, [128]) for t in range(NT)] for k in range(16)]
        H2T = [[mk("H2T%d_%d" % (k, t), "C", k * 2048 + t * 256, BF16, [128]) for t in range(NT)] for k in range(16)]
        CATall = mk("CATall", "C", 0, BF16, [16, TOK]); P.regions["C"].remove(CATall)
        H2Tall = CATall
        LN2G = mk("LN2G", "B", 8192, F32, [2048])

        def wslot(s, shape):
            return mk("W%d_%s" % (s, "x".join(map(str, shape))), "W%d" % s, 0, BF16, shape)
        WA = [wslot(s, [16, 512]) for s in range(4)]
        WF2 = [wslot(s, [4, 2048]) for s in range(4)]
        RA = [mk("RA%d" % i, "W3", i * 2048, F32, [512]) for i in range(2)]
        RBs = [mk("RB%d" % i, "W3", 4096 + i * 2048, F32, [512]) for i in range(2)]
        QR = [mk("QR%d" % i, "W3", 8192 + i * 1024, BF16, [512]) for i in range(3)]
        PT = [mk("PT%d" % i, "W3", i * 5120, BF16, [5, 512]) for i in range(3)]
        GG = [mk("GG%d" % i, "W3", i * 4096, F32, [1024]) for i in range(2)]
        GNB = [mk("GNB%d" % i, "W3", 8192 + i * 2048, BF16, [1024]) for i in range(2)]
        TMPG = mk("TMPG", "W3", 12288, F32, [512])
        LN1G = mk("LN1G", "W3", 0, F32, [2048])
        LN1B = mk("LN1B", "W3", 8192, F32, [2048])

        o = 0
        def dmk(name, dt, shape):
            nonlocal o
            n = int(np.prod(shape)) * (2 if dt == BF16 else 4)
            n = (n + 31) // 32 * 32
            b = mk(name, "D", o, dt, shape)
            o += n
            return b
        BSB = dmk("BSB", F32, [1024])
        GLNG = dmk("GLNG", F32, [1024])
        GLNB = dmk("GLNB", F32, [1024])
        ffn_tmp_end = o
        WST = dmk("WST", BF16, [8, 128])
        MASK = dmk("MASK", BF16, [4, 128])
        IDB = dmk("IDB", BF16, [128])
        IDF = dmk("IDF", F32, [128])
        ONESB = dmk("ONESB", BF16, [128])
        ONESF = dmk("ONESF", F32, [128])
        DG = [dmk("DG%d" % i, F32, [128]) for i in range(2)]
        LF = dmk("LF", BF16, [16, 32])
        SEL4 = dmk("SEL4", F32, [2])
        MT = [dmk("MT%d" % i, F32, [512]) for i in range(2)]
        CV = dmk("CV", F32, [32])
        SV = dmk("SV", BF16, [16, 2])
        BADAT = dmk("BADAT", F32, [96])
        MODCOL = [dmk("MODCOL%d" % i, F32, [16, 2]) for i in range(6)]
        LN1GT = dmk("LN1GT", F32, [16])
        LN1BT = dmk("LN1BT", F32, [16])
        A2 = dmk("A2", F32, [16])
        B2 = dmk("B2", F32, [16])
        SINK8 = dmk("SINK8", F32, [8])
        EXPS = dmk("EXPS", F32, [8])
        EXPSB = mk("EXPSB", "A", 63488, BF16, [1024])
        RDEN = dmk("RDEN", F32, [512])
        ST = [dmk("ST%d" % i, F32, [24]) for i in range(2)]
        MV = [dmk("MV%d" % i, F32, [8]) for i in range(8)]
        GATE2 = [mk("GATE2_%d" % n, "D", n * 2048, F32, [512]) for n in range(4)]
        RELU = [mk("RELU%d" % i, "D", 8192 + i * 2048, F32, [512]) for i in range(2)]
        TMP = [mk("TMP%d" % i, "D", 8192 + i * 2048, F32, [512]) for i in range(2)]
        LN2B = mk("LN2B", "D", WST.lo, F32, [2048])
        assert WST.lo + 8192 <= MT[1].hi
        assert 8192 + 4096 <= ffn_tmp_end

        def dma(queue, out_ap, in_ap, sem, mode, reads=(), writes=()):
            def fn(e):
                return e.dma_start(out=out_ap, in_=in_ap)
            return P.add(queue, fn, reads, writes, dma_sem=sem, dma_mode=mode)

        def act(out_ap, in_ap, func, reads, writes, bias=None, scale=None):
            kw = {}
            if bias is not None:
                kw["bias"] = bias
            if scale is not None:
                kw["scale"] = scale
            def fn(e):
                return e.activation(out=out_ap, in_=in_ap, func=func, **kw)
            return P.add("scalar", fn, reads, writes)

        def vec(fn, reads, writes):
            return P.add("vector", fn, reads, writes)

        def mm_group(ps_ap, pairs, reads, writes):
            def fn(te):
                n = len(pairs)
                ins = None
                for i, (l, r) in enumerate(pairs):
                    ins = te.matmul(ps_ap, lhsT=l, rhs=r, start=(i == 0), stop=(i == n - 1))
                return ins
            return P.add("tensor", fn, reads, writes)

        def transposes(items, reads, writes):
            def fn(te):
                ins = None
                for (o_, i_) in items:
                    ins = te.transpose(o_, i_, IDB.ap)
                return ins
            return P.add("tensor", fn, reads + [IDB], writes)

        def pipeline(n, stage1, stage2, depth=1, hook=None):
            pend = []
            for i in range(n):
                ctx_ = stage1(i)
                pend.append((i, ctx_))
                if len(pend) > depth:
                    j, c_ = pend.pop(0)
                    stage2(j, c_)
                if hook is not None:
                    hook(i)
            for j, c_ in pend:
                stage2(j, c_)

        wstate = {"i": 0, "ffn": False, "hold3": True}

        wpinned = set()

        def wfree(*bufs):
            for b in bufs:
                wpinned.discard(int(b.region[1]))

        def wload(kind, idx):
            if wstate["ffn"]:
                allowed = (0, 1, 2) if wstate["hold3"] else (0, 1, 2, 3)
            else:
                allowed = (0, 1, 2, 3) if wstate["i"] < 9 else (0, 1, 2)
            allowed = [q for q in allowed if q not in wpinned]
            assert allowed, "no free weight slot"
            s = min(allowed, key=lambda q: P.region_last_use("W%d" % q))
            wpinned.add(s)
            wstate["i"] += 1
            if kind == "ada":
                src = w_ada.rearrange("(k p) n -> p k n", p=128)[:, :, idx * 512:(idx + 1) * 512]
                dst = WA[s]
            elif kind == "win":
                src = w_in.rearrange("(k p) n -> p k n", p=128)[:, :, idx * 512:(idx + 1) * 512]
                dst = WA[s]
            elif kind == "wout":
                src = w_out.rearrange("(k p) n -> p k n", p=128)[:, :, idx * 512:(idx + 1) * 512]
                dst = WA[s]
            elif kind == "ff1":
                src = w_ff1.rearrange("(k p) n -> p k n", p=128)[:, :, idx * 512:(idx + 1) * 512]
                dst = WA[s]
            else:
                src = w_ff2.rearrange("(k p) n -> p k n", p=128)[:, idx * 4:(idx + 1) * 4, :]
                dst = WF2[s]
            dma("gpsimd", dst.ap, src, "w%d" % s, "slot", writes=[dst])
            return dst

        def cload(buf, src, queue="sync", grp="a"):
            dma(queue, buf.ap, src, "cst%s_%s" % (grp, queue), "group", writes=[buf])
        cload(CV, cvec)
        cload(IDF, ident)
        cload(SEL4, sel4)
        cload(BADAT, b_adaT)
        cload(CTXT, ctxT.rearrange("(k p) t -> p k t", p=128))

        def late_consts():
            cload(SINK8, sink.partition_broadcast(128), grp="b")
            cload(ROPEC, ropeC.rearrange("(t p) d -> p t d", p=128), grp="b")
            cload(ROPES, ropeS.rearrange("(t p) d -> p t d", p=128), grp="b")
            cload(BSB, b_s.partition_broadcast(128), grp="b")
            cload(GLNG, gln_g.partition_broadcast(128), grp="b")
            cload(GLNB, gln_b.partition_broadcast(128), grp="b")
            cload(LN1GT, ln1_gT, grp="b")
            cload(LN1BT, ln1_bT, grp="b")
        cload(IDB, ident, "gpsimd")
        cload(MASK, masks.rearrange("p (a b) -> p a b", a=4), "gpsimd")
        cload(WST, w_sT.rearrange("p (a b) -> p a b", a=8), "gpsimd")
        vec(lambda e: e.memset(ONESB.ap, 1.0), [], [ONESB])
        vec(lambda e: e.memset(ONESF.ap, 1.0), [], [ONESF])

        act(SV.ap, CV.ap.rearrange("p (a b) -> p a b", a=16), AF.Silu, [CV], [SV])
        vec(lambda e: e.tensor_copy(out=LB.ap[:, :, 0:16], in_=SV.ap[:, :, 0:1].to_broadcast([128, 16, 16])), [SV], [LB])
        vec(lambda e: e.tensor_copy(out=LB.ap[:, :, 16:32], in_=SV.ap[:, :, 1:2].to_broadcast([128, 16, 16])), [SV], [LB])
        vec(lambda e: e.tensor_copy(out=LF.ap, in_=SV.ap[:, :, 0:1].to_broadcast([128, 16, 32])), [SV], [LF])

        ada_pend = []
        ada_cnt = [0]

        def ada_stage2(n, pb, mt):
            chunk, q = n // 4, n % 4
            pb2 = bank()
            def fn(te):
                ins = None
                for i in range(4):
                    ins = te.matmul(pb2.ap[:, 2 * i:2 * i + 2], lhsT=mt.ap[:, i * 128:(i + 1) * 128],
                                    rhs=SEL4.ap, start=True, stop=True)
                return ins
            P.add("tensor", fn, [mt, SEL4], [pb2])
            mc = MODCOL[chunk]
            def fn2(e):
                return e.tensor_tensor(
                    out=mc.ap[:, q * 4:q * 4 + 4, :],
                    in0=pb2.ap[:, 0:8].rearrange("p (a b) -> p a b", a=4),
                    in1=BADAT.ap[:, chunk * 16 + q * 4:chunk * 16 + q * 4 + 4].unsqueeze(2).to_broadcast([128, 4, 2]),
                    op=ALU.add)
            vec(fn2, [pb2, BADAT], [mc])

        def ada_flush():
            while ada_pend:
                ada_stage2(*ada_pend.pop(0))

        def ada_chunk(n):
            W = wload("ada", n)
            lhs = LB if n < 8 else LF
            pb = bank()
            mt = MT[ada_cnt[0] % 2]
            ada_cnt[0] += 1
            def fng(te):
                ins = None
                for i in range(4):
                    for g in range(4):
                        k = 4 * g + i
                        ins = te.matmul(pb.ap[32 * g:32 * g + 32, :], lhsT=lhs.ap[:, k, :], rhs=W.ap[:, k, :],
                                        start=(i == 0), stop=(i == 3), tile_position=(0, 32 * g))
                return ins
            P.add("tensor", fng, [lhs, W], [pb])
            wfree(W)
            vec(lambda e: e.tensor_copy(out=mt.ap, in_=pb.ap), [pb], [mt])
            ada_flush()
            ada_pend.append((n, pb, mt))

        for k in range(16):
            xs = XTR[k % 3]
        xt_issued = [0]

        def issue_xt(k):
            xs = XTR[k % 3]
            dma("sync", xs.ap, xT[k * 128:(k + 1) * 128, :], "xt%d" % (k % 3), "slot", writes=[xs])

        for q in range(4):
            ada_chunk(q)
            ada_chunk(4 + q)
            ada_flush()
            sl = slice(4 * q, 4 * q + 4)
            vec(lambda e, sl=sl: e.tensor_scalar(out=MODCOL[1].ap[:, sl, :], in0=MODCOL[1].ap[:, sl, :], scalar1=1.0, scalar2=None, op0=ALU.add),
                [MODCOL[1]], [MODCOL[1]])
            for k in range(4 * q, 4 * q + 4):
                act(HCT[k].ap, CTXT.ap[:, k, :], AF.Identity, [CTXT, MODCOL[0], MODCOL[1]], [HCT[k]],
                    bias=MODCOL[0].ap[:, k, 1:2], scale=MODCOL[1].ap[:, k, 1:2])
            for k in range(4 * q, 4 * q + 4):
                xs = XTR[k % 3]
                issue_xt(k)
                act(HT[k].ap, xs.ap, AF.Identity, [xs, MODCOL[0], MODCOL[1]], [HT[k]],
                    bias=MODCOL[0].ap[:, k, 0:1], scale=MODCOL[1].ap[:, k, 0:1])
            if q == 2:
                late_consts()

        rope_i = [0]

        def rope(pb, col0, nh, tpos, qr):
            i = rope_i[0] % 2
            rope_i[0] += 1
            ra, rb = RA[i], RBs[i]
            n = nh * 128
            src = pb.ap[:, col0:col0 + n]
            src3 = src.rearrange("p (h d) -> p h d", h=nh)
            src5 = src.rearrange("p (h f j d) -> p h f j d", h=nh, f=2, j=2)
            rb5 = rb.ap[:, 0:n].rearrange("p (h f j d) -> p h f j d", h=nh, f=2, j=2)
            s4 = ROPES.ap[:, tpos, :].rearrange("p (f j d) -> p f j d", f=2, j=2)
            cb = ROPEC.ap[:, tpos, :].unsqueeze(1).to_broadcast([128, nh, 128])
            vec(lambda e: e.tensor_tensor(out=ra.ap[:, 0:n].rearrange("p (h d) -> p h d", h=nh), in0=src3, in1=cb, op=ALU.mult),
                [pb, ROPEC], [ra])
            vec(lambda e: e.tensor_tensor(out=rb5[:, :, :, 0, :], in0=src5[:, :, :, 1, :],
                                          in1=s4[:, :, 0, :].unsqueeze(1).to_broadcast([128, nh, 2, 32]), op=ALU.mult),
                [pb, ROPES], [rb])
            vec(lambda e: e.tensor_tensor(out=rb5[:, :, :, 1, :], in0=src5[:, :, :, 0, :],
                                          in1=s4[:, :, 1, :].unsqueeze(1).to_broadcast([128, nh, 2, 32]), op=ALU.mult),
                [pb, ROPES], [rb])
            vec(lambda e: e.tensor_tensor(out=qr.ap[:, 0:n], in0=ra.ap[:, 0:n], in1=rb.ap[:, 0:n], op=ALU.add),
                [ra, rb], [qr])

        W = wload("win", 2)
        for hk in range(2):
            pb = bank()
            mm_group(pb.ap[:, 0:256], [(W.ap[:, k, hk * 128:(hk + 1) * 128], HCT[k].ap) for k in range(16)], [W] + HCT, [pb])
            act(KCT.ap[:, hk, :], pb.ap[:, 0:256], AF.Copy, [pb], [KCT])
        for tb in range(2):
            pb = bank()
            mm_group(pb.ap[:, 0:256], [(HCT[k].ap[:, tb * 128:(tb + 1) * 128], W.ap[:, k, 256:512]) for k in range(16)], [W] + HCT, [pb])
            act(VC.ap[:, tb, :], pb.ap[:, 0:256], AF.Copy, [pb], [VC])

        def kv_s1(t, W=W):
            pb = bank()
            mm_group(pb.ap, [(HT[k].ap[:, t * 128:(t + 1) * 128], W.ap[:, k, :]) for k in range(16)], [W] + HT, [pb])
            qr = QR[t % 3]
            rope(pb, 0, 2, t, qr)
            vec(lambda e: e.tensor_copy(out=Vb[t].ap, in_=pb.ap[:, 256:512]), [pb], [Vb[t]])
            return qr

        def kv_s2(t, qr):
            pt_ = bank()
            ptb = pt_.ap.bitcast(BF16)
            transposes([(ptb[:, h * 128:(h + 1) * 128], qr.ap[:, h * 128:(h + 1) * 128]) for h in range(2)], [qr], [pt_])
            for h in range(2):
                act(KT[h].ap[:, t * 128:(t + 1) * 128], ptb[:, h * 128:(h + 1) * 128], AF.Copy, [pt_], [KT[h]])
        pipeline(NH, kv_s1, kv_s2, depth=2)
        wfree(W)
        ada_chunk(8)
        ada_chunk(9)

        for c in range(2):
            W = wload("win", c)

            def q_s1(t, W=W):
                pb = bank()
                mm_group(pb.ap, [(HT[k].ap[:, (t + 1) * 128:(t + 2) * 128], W.ap[:, k, :]) for k in range(16)], [W] + HT, [pb])
                qr = QR[t % 3]
                rope(pb, 0, 4, t + 1, qr)
                return qr

            def q_s2(t, qr, c=c):
                pt_ = bank()
                ptb = pt_.ap.bitcast(BF16)
                transposes([(ptb[:, h * 128:(h + 1) * 128], qr.ap[:, h * 128:(h + 1) * 128]) for h in range(4)], [qr], [pt_])
                act(QTall.ap[:, 4 * c:4 * c + 4, t * 128:(t + 1) * 128], ptb[:, 0:512].rearrange("p (h q) -> p h q", h=4), AF.Copy,
                    [pt_], QT[4 * c:4 * c + 4])
            pipeline(NT, q_s1, q_s2, depth=2)
            wfree(W)
            ada_chunk(10 + 2 * c)
            ada_chunk(11 + 2 * c)

        act(EXPS.ap, SINK8.ap, AF.Exp, [SINK8], [EXPS])
        vec(lambda e: e.tensor_copy(out=EXPSB.ap[0:1, :].rearrange("p (h q) -> p h q", h=8),
                                    in_=EXPS.ap[0:1, :].unsqueeze(2).to_broadcast([1, 8, 128])), [EXPS], [EXPSB])

        def at_s1(it):
            t, hk = it // 2, it % 2
            pt = PT[it % 3]
            qheads = QT[4 * hk:4 * hk + 4]
            rhs_q = QTall.ap[:, 4 * hk:4 * hk + 4, t * 128:(t + 1) * 128]
            sb_ = []
            for j in range(5):
                pb = bank()
                if j < 3:
                    kk = KT[hk].ap[:, (t + j) * 128:(t + j + 1) * 128]
                    rd = [KT[hk]]
                else:
                    kk = KCT.ap[:, hk, (j - 3) * 128:(j - 2) * 128]
                    rd = [KCT]
                mm_group(pb.ap.rearrange("p (h q) -> p h q", h=4), [(kk, rhs_q)], rd + qheads, [pb])
                sb_.append(pb)
            for j in range(5):
                act(pt.ap[:, j, :], sb_[j].ap, AF.Exp, [sb_[j]], [pt], scale=QSCALE)
            mL = 2 if t == 0 else 0
            mU = 3 if t == NT - 1 else 1
            for (j, mi) in ((0, mL), (2, mU)):
                def fnm(e, j=j, mi=mi, pt=pt):
                    v3 = pt.ap[:, j, :].rearrange("p (h q) -> p h q", h=4)
                    return e.tensor_tensor(out=v3, in0=v3, in1=MASK.ap[:, mi, :].unsqueeze(1).to_broadcast([128, 4, 128]), op=ALU.mult)
                vec(fnm, [pt, MASK], [pt])
            return pt

        def at_s2(it, pt):
            t, hk = it // 2, it % 2
            po = bank()
            pd = bank()
            pairs = []
            rds = [pt, VC]
            for j in range(5):
                if j < 3:
                    vv = Vb[t + j].ap[:, hk * 128:(hk + 1) * 128]
                    rds.append(Vb[t + j])
                else:
                    vv = VC.ap[:, j - 3, hk * 128:(hk + 1) * 128]
                pairs.append((vv, pt.ap[:, j, :]))
            mm_group(po.ap, pairs, rds, [po])
            mm_group(pd.ap, [(ONESB.ap, pt.ap[:, j, :]) for j in range(5)] + [(ONESB.ap[0:1, :], EXPSB.ap[0:1, hk * 512:(hk + 1) * 512])],
                     [pt, ONESB, EXPSB], [pd])
            vec(lambda e: e.reciprocal(out=RDEN.ap, in_=pd.ap), [pd], [RDEN])
            def fno(e):
                return e.tensor_tensor(out=CATall.ap[:, 4 * hk:4 * hk + 4, t * 128:(t + 1) * 128],
                                       in0=po.ap.rearrange("p (h q) -> p h q", h=4),
                                       in1=RDEN.ap.rearrange("p (h q) -> p h q", h=4), op=ALU.mult)
            vec(fno, [po, RDEN], [CAT[4 * hk + h][t] for h in range(4)])

        at_ada = {1: 14, 4: 15, 7: 16, 10: 17, 12: 18, 14: 19}

        def at_hook(it):
            if it in at_ada:
                ada_chunk(at_ada[it])
        pipeline(2 * NT, at_s1, at_s2, depth=2, hook=at_hook)

        for c in (3, 4):
            W = wload("win", c)
            for m in range(4):
                h = (c - 3) * 4 + m
                for tg in range(2):
                    pb = bank()
                    mm_group(pb.ap, [(W.ap[:, k, m * 128:(m + 1) * 128], HT[k].ap[:, 128 + tg * 512:128 + (tg + 1) * 512]) for k in range(16)],
                             [W] + HT, [pb])
                    act(GUT[h].ap[:, tg * 512:(tg + 1) * 512], pb.ap, AF.Gelu_apprx_tanh, [pb], [GUT[h]])
            wfree(W)
        ada_flush()

        Wg = [wload("win", 5), wload("win", 6)]
        GUTall = QTall

        def g_s1a(t):
            gg = GG[t % 2]
            st, mv = ST[t % 2], MV[t % 8]
            pbs = []
            for half in range(2):
                pb = bank()
                mm_group(pb.ap, [(HT[k].ap[:, (t + 1) * 128:(t + 2) * 128], Wg[half].ap[:, k, :]) for k in range(16)], [Wg[half]] + HT, [pb])
                pbs.append(pb)
            for half in range(2):
                act(gg.ap[:, half * 512:(half + 1) * 512], pbs[half].ap, AF.Gelu_apprx_tanh, [pbs[half]], [gg])
            for half in range(2):
                vec(lambda e, half=half: e.bn_stats(out=st.ap[:, half * 6:(half + 1) * 6], in_=gg.ap[:, half * 512:(half + 1) * 512]), [gg], [st])
            vec(lambda e: e.bn_aggr(out=mv.ap[:, 0:2], in_=st.ap[:, 0:12]), [st], [mv])
            vec(lambda e: e.tensor_scalar(out=mv.ap[:, 2:3], in0=mv.ap[:, 1:2], scalar1=EPS, scalar2=None, op0=ALU.add), [mv], [mv])
            act(mv.ap[:, 3:4], mv.ap[:, 2:3], AF.Sqrt, [mv], [mv])

        def g_s1b(t):
            gg = GG[t % 2]
            gnb = GNB[t % 2]
            mv = MV[t % 8]
            vec(lambda e: e.reciprocal(out=mv.ap[:, 4:5], in_=mv.ap[:, 3:4]), [mv], [mv])
            vec(lambda e: e.scalar_tensor_tensor(out=gg.ap, in0=gg.ap, scalar=mv.ap[:, 0:1], in1=GLNG.ap, op0=ALU.subtract, op1=ALU.mult),
                [gg, mv, GLNG], [gg])
            vec(lambda e: e.scalar_tensor_tensor(out=gnb.ap, in0=gg.ap, scalar=mv.ap[:, 4:5], in1=GLNB.ap, op0=ALU.mult, op1=ALU.add),
                [gg, mv, GLNB], [gnb])
            return gnb

        def g_s2(t, gnb):
            for i in range(2):
                pb = bank()
                def fnm(te, pb=pb, i=i):
                    ins = None
                    for hh in range(4):
                        h = 4 * i + hh
                        ins = te.matmul(pb.ap[:, hh * 128:(hh + 1) * 128], lhsT=gnb.ap[:, h * 128:(h + 1) * 128], rhs=WST.ap[:, h, :], start=True, stop=True)
                    return ins
                P.add("tensor", fnm, [gnb, WST], [pb])
                vec(lambda e, pb=pb, i=i: e.tensor_tensor(out=TMPG.ap, in0=pb.ap, in1=BSB.ap[:, i * 512:(i + 1) * 512], op=ALU.add), [pb, BSB], [TMPG])
                def fnc(e, i=i):
                    return e.tensor_tensor(out=CATall.ap[:, 8 + 4 * i:12 + 4 * i, t * 128:(t + 1) * 128],
                                           in0=TMPG.ap.rearrange("p (h q) -> p h q", h=4),
                                           in1=GUTall.ap[:, 4 * i:4 * i + 4, t * 128:(t + 1) * 128], op=ALU.mult)
                vec(fnc, [TMPG] + GUT[4 * i:4 * i + 4], [CAT[8 + 4 * i + h][t] for h in range(4)])
        wout_pend = []
        gate1_pending = [True]

        def wout_group(W, n, t):
            pb = bank()
            tmp = TMP[(t + n) % 2]
            mm_group(pb.ap, [(CATall.ap[:, k, t * 128:(t + 1) * 128], W.ap[:, k, :]) for k in range(16)],
                     [W] + [CAT[k][t] for k in range(16)], [pb])
            xb = X[t][n]
            bi = PSB.index(pb)

            def evac():
                bank_reserved.discard(bi)
                vec(lambda e: e.tensor_tensor(out=tmp.ap, in0=pb.ap, in1=GATE1[n].ap, op=ALU.mult), [pb, GATE1[n]], [tmp])
                vec(lambda e: e.scalar_tensor_tensor(out=xb.ap, in0=xb.ap, scalar=ALPHA, in1=tmp.ap, op0=ALU.mult, op1=ALU.add), [xb, tmp], [xb])
            if gate1_pending[0]:
                bank_reserved.add(bi)
                wout_pend.append(evac)
            else:
                evac()

        Wo0 = [None]
        for t in range(NT + 2):
            if t < NT:
                g_s1a(t)
            if 1 <= t <= NT:
                g_s1b(t - 1)
            if t == NT:
                Wo0[0] = wload("wout", 0)
                wout_group(Wo0[0], 0, 0)
                wout_group(Wo0[0], 0, 1)
            if t == NT + 1:
                wout_group(Wo0[0], 0, 2)
                wout_group(Wo0[0], 0, 3)
            if t >= 2:
                g_s2(t - 2, GNB[(t - 2) % 2])
            if t in (2, 5):
                ada_chunk(20 + (t - 2) // 3)
        wfree(*Wg)

        vec(lambda e: e.tensor_scalar(out=MODCOL[4].ap, in0=MODCOL[4].ap, scalar1=1.0, scalar2=None, op0=ALU.add), [MODCOL[4]], [MODCOL[4]])
        vec(lambda e: e.tensor_tensor(out=A2.ap, in0=LN1GT.ap, in1=MODCOL[4].ap[:, :, 0], op=ALU.mult), [LN1GT, MODCOL[4]], [A2])
        vec(lambda e: e.tensor_tensor(out=B2.ap, in0=LN1BT.ap, in1=MODCOL[4].ap[:, :, 0], op=ALU.mult), [LN1BT, MODCOL[4]], [B2])
        vec(lambda e: e.tensor_tensor(out=B2.ap, in0=B2.ap, in1=MODCOL[3].ap[:, :, 0], op=ALU.add), [B2, MODCOL[3]], [B2])

        def gate_bcast(mc, G):
            for n in range(4):
                pb = bank()
                for i in range(4):
                    kc = 4 * n + i
                    dg = DG[kc % 2]
                    vec(lambda e, dg=dg, kc=kc: e.tensor_scalar(out=dg.ap, in0=IDF.ap, scalar1=mc.ap[:, kc, 0:1], scalar2=None, op0=ALU.mult),
                        [IDF, mc], [dg])
                    def fn(te, pb=pb, i=i, dg=dg):
                        return te.matmul(pb.ap[:, i * 128:(i + 1) * 128], lhsT=ONESF.ap, rhs=dg.ap, start=True, stop=True)
                    P.add("tensor", fn, [ONESF, dg], [pb])
                vec(lambda e, pb=pb, n=n: e.tensor_copy(out=G[n].ap, in_=pb.ap), [pb], [G[n]])

        Xrow = lambda t: Xall.ap[:, t, :]

        def ln_stats_a(t):
            st, mv = ST[t % 2], MV[t % 8]
            for c in range(4):
                vec(lambda e, c=c: e.bn_stats(out=st.ap[:, c * 6:(c + 1) * 6], in_=X[t][c].ap), [X[t][c]], [st])
            vec(lambda e: e.bn_aggr(out=mv.ap[:, 0:2], in_=st.ap), [st], [mv])
            vec(lambda e: e.tensor_scalar(out=mv.ap[:, 2:3], in0=mv.ap[:, 1:2], scalar1=EPS, scalar2=None, op0=ALU.add), [mv], [mv])
            act(mv.ap[:, 3:4], mv.ap[:, 2:3], AF.Sqrt, [mv], [mv])
            return mv

        def ln_stats_b(t):
            mv = MV[t % 8]
            vec(lambda e: e.reciprocal(out=mv.ap[:, 4:5], in_=mv.ap[:, 3:4]), [mv], [mv])
            return mv

        def ln_stats(t):
            ln_stats_a(t)
            return ln_stats_b(t)

        def ln_apply(t, mv, G, Bb, eng="vector"):
            if eng == "vector":
                vec(lambda e: e.scalar_tensor_tensor(out=Xrow(t), in0=Xrow(t), scalar=mv.ap[:, 0:1], in1=G.ap, op0=ALU.subtract, op1=ALU.mult),
                    X[t] + [mv, G], X[t])
                vec(lambda e: e.scalar_tensor_tensor(out=Xrow(t), in0=Xrow(t), scalar=mv.ap[:, 4:5], in1=Bb.ap, op0=ALU.mult, op1=ALU.add),
                    X[t] + [mv, Bb], X[t])
            else:
                P.add(eng, lambda e: e.tensor_scalar(out=Xrow(t), in0=Xrow(t), scalar1=mv.ap[:, 0:1], scalar2=mv.ap[:, 4:5], op0=ALU.subtract, op1=ALU.mult),
                      X[t] + [mv], X[t])
                P.add(eng, lambda e: e.tensor_tensor(out=Xrow(t), in0=Xrow(t), in1=G.ap, op=ALU.mult), X[t] + [G], X[t])
                P.add(eng, lambda e: e.tensor_tensor(out=Xrow(t), in0=Xrow(t), in1=Bb.ap, op=ALU.add), X[t] + [Bb], X[t])

        def ln1_x1(t):
            ln_apply(t, MV[t % 8], LN1G, LN1B)

        def ln1_zb(t):
            mv = MV[t % 8]
            zb = ZB[t % 2]
            vec(lambda e: e.tensor_scalar(out=zb.ap, in0=Xrow(t), scalar1=mv.ap[:, 0:1], scalar2=mv.ap[:, 4:5], op0=ALU.subtract, op1=ALU.mult),
                X[t] + [mv], [zb])

        def ln1_tr(t):
            zb = ZB[t % 2]
            for i in range(2):
                pt_ = bank()
                ptb = pt_.ap.bitcast(BF16)
                transposes([(ptb[:, j * 128:(j + 1) * 128], zb.ap[:, (8 * i + j) * 128:(8 * i + j + 1) * 128]) for j in range(8)], [zb], [pt_])
                for j in range(8):
                    k = 8 * i + j
                    act(H2Tall.ap[:, k, t * 128:(t + 1) * 128], ptb[:, j * 128:(j + 1) * 128], AF.Identity, [pt_, A2, B2], [H2T[k][t]],
                        bias=B2.ap[:, k:k + 1], scale=A2.ap[:, k:k + 1])

        for n in range(4):
            wr = [X[t][n] for t in range(NT)]
            dma("sync", Xall.ap[:, :, n * 512:(n + 1) * 512], x_tm.rearrange("(t p) c -> p t c", p=128)[:, :, n * 512:(n + 1) * 512],
                "x%d" % n, "slot", writes=wr)
        dma("sync", LN1G.ap, ln1_g.partition_broadcast(128), "ln1p", "group", writes=[LN1G])
        dma("sync", LN1B.ap, ln1_b.partition_broadcast(128), "ln1p", "group", writes=[LN1B])
        for n in range(3):
            W = Wo0[0] if n == 0 else wload("wout", n)
            for t in range(NT):
                if n == 0 and t < 4:
                    continue
                if n == 0 and t == 4:
                    pend_ev = wout_pend[:]
                    del wout_pend[:]
                    gate_bcast(MODCOL[2], GATE1)
                    for f in pend_ev:
                        f()
                    gate1_pending[0] = False
                wout_group(W, n, t)
            wfree(W)

        W = wload("wout", 3)
        for t in range(NT):
            wout_group(W, 3, t)
            ln_stats_a(t)
            if t >= 1:
                ln_stats_b(t - 1)
                ln1_zb(t - 1)
            if t >= 2:
                ln1_tr(t - 2)
        wfree(W)
        ln_stats_b(NT - 1)
        ln1_zb(NT - 1)
        ln1_tr(NT - 2)

        wstate["ffn"] = True
        wstate["i"] = 0
        ln2_loaded = [False]
        relu_i = [0]
        def ffn1_group(W, jj, m, tg, hg=None, blocks=None, sq_on_act=False):
            b0, nb = (4 * tg, 4) if blocks is None else blocks
            ncol = nb * 128
            hk_ = HID[jj * 4 + m][tg if hg is None else hg]
            pb = bank()
            mm_group(pb.ap[:, 0:ncol], [(W.ap[:, k, m * 128:(m + 1) * 128], H2Tall.ap[:, k, b0 * 128:b0 * 128 + ncol]) for k in range(16)],
                     [W] + [H2T[k][t] for k in range(16) for t in range(b0, b0 + nb)], [pb])
            rl = RELU[relu_i[0] % 2]
            relu_i[0] += 1
            act(rl.ap[:, 0:ncol], pb.ap[:, 0:ncol], AF.Relu, [pb], [rl])
            if sq_on_act:
                act(hk_.ap[:, 0:ncol], rl.ap[:, 0:ncol], AF.Square, [rl], [hk_])
            else:
                vec(lambda e: e.tensor_tensor(out=hk_.ap[:, 0:ncol], in0=rl.ap[:, 0:ncol], in1=rl.ap[:, 0:ncol], op=ALU.mult), [rl], [hk_])

        def ffn2_block(p, t, W2, hg=None, tt=None):
            g = t // 4 if hg is None else hg
            tt = t % 4 if tt is None else tt
            for n in range(4):
                pb = bank()
                tmp = TMP[n % 2]
                mm_group(pb.ap, [(HID[kc][g].ap[:, tt * 128:(tt + 1) * 128], W2[kc // 4].ap[:, kc % 4, n * 512:(n + 1) * 512]) for kc in range(8)],
                         W2 + [HID[kc][g] for kc in range(8)], [pb])
                vec(lambda e, pb=pb, n=n, tmp=tmp: e.tensor_tensor(out=tmp.ap, in0=pb.ap, in1=GATE2[n].ap, op=ALU.mult), [pb, GATE2[n]], [tmp])
                xb = X[t][n]
                if p == 0:
                    vec(lambda e, xb=xb, tmp=tmp: e.scalar_tensor_tensor(out=xb.ap, in0=xb.ap, scalar=ALPHA, in1=tmp.ap, op0=ALU.mult, op1=ALU.add), [xb, tmp], [xb])
                else:
                    vec(lambda e, xb=xb, tmp=tmp: e.tensor_tensor(out=xb.ap, in0=xb.ap, in1=tmp.ap, op=ALU.add), [xb, tmp], [xb])

        def ln2_finish(t):
            mv = ln_stats_b(t)
            ln_apply(t, mv, LN2G, LN2B)
            dma("sync", y[t * 128:(t + 1) * 128, :], Xrow(t), "out", "group", reads=X[t])

        for p in range(8):
            if p == 0:
                W1 = [wload("ff1", 0), wload("ff1", 1)]
                gi = 0
                for tg in range(2):
                    for jj in range(2):
                        for m in range(4):
                            if gi == 0:
                                ln1_tr(NT - 1)
                            ffn1_group(W1[jj], jj, m, tg)
                            if gi < NT:
                                ln1_x1(gi)
                            elif gi in (9, 14):
                                ada_chunk(22 + (gi - 9) // 5)
                            gi += 1
                wfree(*W1)
                wstate["hold3"] = False
                ada_flush()
                gate_bcast(MODCOL[5], GATE2)
                W2 = [wload("ff2", 0), wload("ff2", 1)]
                for t in range(NT):
                    ffn2_block(p, t, W2)
                wfree(*W2)
            elif p < 7:
                for jj in range(2):
                    W = wload("ff1", 2 * p + jj)
                    for m in range(4):
                        for tg in range(2):
                            ffn1_group(W, jj, m, tg)
                    wfree(W)
                W2 = [wload("ff2", 2 * p), wload("ff2", 2 * p + 1)]
                for t in range(NT):
                    ffn2_block(p, t, W2)
                wfree(*W2)
            else:
                W1 = [wload("ff1", 2 * p), wload("ff1", 2 * p + 1)]
                W2 = [wload("ff2", 2 * p), wload("ff2", 2 * p + 1)]
                dma("sync", LN2G.ap, ln2_g.partition_broadcast(128), "ln2p", "group", writes=[LN2G])
                dma("sync", LN2B.ap, ln2_b.partition_broadcast(128), "ln2p", "group", writes=[LN2B])
                for (b0, nb) in ((0, 4), (4, 2), (6, 2)):
                    for jj in range(2):
                        for m in range(4):
                            ffn1_group(W1[jj], jj, m, None, hg=0, blocks=(b0, nb), sq_on_act=True)
                    for t in range(b0, b0 + nb):
                        ffn2_block(p, t, W2, hg=0, tt=t - b0)
                        ln_stats_a(t)
                        if t >= 1:
                            ln2_finish(t - 1)
                ln2_finish(NT - 1)

        P.finalize()
        sem_names = ["eng:" + e for e in ENGS] + ["dma:" + s for s in P.dma_sems]
        sems = {}
        for sn in sem_names:
            sems[sn] = es.enter_context(nc.semaphore(sn.replace(":", "_")))
        block = es.enter_context(nc.Block())
        final_waits = [("dma:out", P.dma_cnt["out"])]
        if DEBUG and "dbg" in P.dma_cnt:
            final_waits.append(("dma:dbg", P.dma_cnt["dbg"]))

        with nc.allow_low_precision("bf16 matmuls, fp32 accumulation"):
            @block.sync
            def _(e):
                P.emit("sync", e, sems)
                for sn, v in final_waits:
                    e.wait_ge(sems[sn], v)

            @block.gpsimd
            def _(e):
                P.emit("gpsimd", e, sems)

            @block.tensor
            def _(e):
                P.emit("tensor", e, sems)

            @block.scalar
            def _(e):
                P.emit("scalar", e, sems)

            @block.vector
            def _(e):
                P.emit("vector", e, sems)
    return nc, list(dbg.keys())


def _rope_tables():
    pos = np.arange(-128, SEQ + 128)
    row = (pos // 64).astype(np.float64)
    col = (pos % 64).astype(np.float64)
    inv = 10000.0 ** (-np.arange(32, dtype=np.float64) / 32.0)
    ar = row[:, None] * inv[None, :]
    ac = col[:, None] * inv[None, :]
    cr, sr, cc, sc = np.cos(ar), np.sin(ar), np.cos(ac), np.sin(ac)
    C = np.concatenate([cr, cr, cc, cc], axis=1).astype(np.float32)
    S = np.concatenate([-sr, sr, -sc, sc], axis=1).astype(np.float32)
    return C, S


_CACHE = {}


def kernel(x, c, ctx, c_ctx, w_ada, b_ada, w_in, attn_sink, gmlp_ln_g, gmlp_ln_b,
           gmlp_w_s, gmlp_b_s, w_out, ln1_g, ln1_b, w_ff1, w_ff2, ln2_g, ln2_b):
    f = lambda a: np.ascontiguousarray(np.asarray(a, dtype=np.float32))
    x, c, ctx, c_ctx = f(x), f(c), f(ctx), f(c_ctx)
    if "nc" not in _CACHE:
        _CACHE["nc"] = build_program()
    nc, dbg_names = _CACHE["nc"]

    C, S = _rope_tables()
    jj = np.arange(128)[:, None]
    ii = np.arange(128)[None, :]
    maskL = (jj >= ii).astype(np.float32)
    maskU = (jj <= ii).astype(np.float32)
    zero = np.zeros((128, 128), np.float32)
    ident = np.eye(128, dtype=np.float32)
    shared = {
        "w_ada": f(w_ada[0]), "b_ada": f(b_ada[0])[None, :],
        "b_adaT": f(np.asarray(b_ada[0]).reshape(96, 128).T),
        "w_in": f(w_in[0]), "sink": f(attn_sink[0])[None, :],
        "gln_g": f(gmlp_ln_g[0])[None, :], "gln_b": f(gmlp_ln_b[0])[None, :],
        "w_sT": f(np.asarray(gmlp_w_s[0]).transpose(2, 0, 1).reshape(128, 1024)),
        "b_s": f(np.asarray(gmlp_b_s[0]).reshape(1, 1024)),
        "w_out": f(w_out[0]),
        "ln1_g": f(ln1_g[0])[None, :], "ln1_b": f(ln1_b[0])[None, :],
        "ln1_gT": f(np.asarray(ln1_g[0]).reshape(16, 128).T), "ln1_bT": f(np.asarray(ln1_b[0]).reshape(16, 128).T),
        "w_ff1": f(w_ff1[0]), "w_ff2": f(w_ff2[0]),
        "ln2_g": f(ln2_g[0])[None, :], "ln2_b": f(ln2_b[0])[None, :],
        "ident": ident,
        "sel4": np.ascontiguousarray(np.stack([(np.arange(128) % 32 == 0), (np.arange(128) % 32 == 16)], axis=1).astype(np.float32)),
    }
    in_maps = []
    for r in range(NCORES):
        b = r // 4
        s0 = (r % 4) * TOK
        xp = np.zeros((NH * 128, D), np.float32)
        lo, hi = s0 - 128, s0 + TOK + 128
        a0, a1 = max(lo, 0), min(hi, SEQ)
        xp[a0 - lo:a1 - lo] = x[b, a0:a1]
        m = dict(shared)
        m["x_tm"] = np.ascontiguousarray(x[b, s0:s0 + TOK])
        m["xT"] = np.ascontiguousarray(xp.T)
        m["ctxT"] = np.ascontiguousarray(ctx[b].T)
        m["cvec"] = np.ascontiguousarray(np.stack([c[b], c_ctx], axis=0).reshape(2, 16, 128).transpose(2, 1, 0).reshape(128, 32))
        m["ropeC"] = np.ascontiguousarray(C[s0:s0 + NH * 128])
        m["ropeS"] = np.ascontiguousarray(S[s0:s0 + NH * 128])
        mk_ = np.stack([maskL, maskU, zero if s0 == 0 else maskL, zero if s0 + TOK == SEQ else maskU], axis=1)
        m["masks"] = np.ascontiguousarray(mk_.reshape(128, 512))
        in_maps.append(m)
    res = run_bass_kernel_spmd(nc, in_maps, core_ids=list(range(NCORES)))
    out = np.empty((2, SEQ, D), np.float32)
    for r in range(NCORES):
        b = r // 4
        s0 = (r % 4) * TOK
        out[b, s0:s0 + TOK] = res.results[r]["y"]
    if DEBUG:
        _CACHE["dbg"] = [{k: res.results[r]["dbg_" + k] for k in dbg_names} for r in range(NCORES)]
    return out
```

```python
import contextlib
import numpy as np
import concourse.bass as bass
import concourse.mybir as mybir
from concourse.bass_utils import run_bass_kernel_spmd

F32 = mybir.dt.float32
BF16 = mybir.dt.bfloat16
AF = mybir.ActivationFunctionType
ALU = mybir.AluOpType

NCORES = 8
D = 2048
SEQ = 4096
TOK = 1024
NT = 8
NH = 10
CTX = 256
DFF = 8192
ALPHA = float(2.0 ** 0.25)
EPS = 1e-5
QSCALE = float(128.0 ** -0.5)
DEBUG = False

ENGS = ["sync", "gpsimd", "tensor", "scalar", "vector"]


class Buf:
    __slots__ = ("name", "region", "lo", "hi", "ap", "last_w", "last_r")

    def __init__(self, name, region, lo, hi, ap):
        self.name, self.region, self.lo, self.hi, self.ap = name, region, lo, hi, ap
        self.last_w = None
        self.last_r = {}


class Op:
    __slots__ = ("eng", "fn", "deps", "needed", "val", "dma_sem", "dma_mode", "idx")

    def __init__(self, eng, fn, dma_sem=None, dma_mode=None):
        self.eng, self.fn = eng, fn
        self.deps = []
        self.needed = False
        self.val = None
        self.dma_sem, self.dma_mode = dma_sem, dma_mode


class Plan:
    def __init__(self):
        self.ops = {e: [] for e in ENGS}
        self.regions = {}
        self.dma_cnt = {}
        self.dma_sems = []
        self.bank_i = 0
        self.nops = 0

    def buf(self, name, region, lo, nbytes, ap):
        b = Buf(name, region, lo, lo + nbytes, ap)
        self.regions.setdefault(region, []).append(b)
        return b

    def _key(self, op):
        return ("dma", op.dma_sem) if op.dma_sem else op.eng

    def add(self, eng, fn, reads=(), writes=(), dma_sem=None, dma_mode=None):
        op = Op(eng, fn, dma_sem, dma_mode)
        op.idx = self.nops
        self.nops += 1
        deps = {}
        for b in reads:
            if b.last_w is not None:
                deps[id(b.last_w)] = b.last_w
            if b.region == "PS":
                for r in b.last_r.values():
                    if r.eng != eng:
                        deps[id(r)] = r
        for b in writes:
            for o in self.regions[b.region]:
                if o.hi <= b.lo or o.lo >= b.hi:
                    continue
                if o.last_w is not None:
                    deps[id(o.last_w)] = o.last_w
                for r in o.last_r.values():
                    deps[id(r)] = r
                if o is not b:
                    o.last_w = None
                    o.last_r = {}
        for d in deps.values():
            if d is op:
                continue
            if d.dma_sem is None and d.eng == eng and dma_sem is None and eng == "tensor":
                continue
            op.deps.append(d)
            d.needed = True
        for b in writes:
            b.last_w = op
            b.last_r = {}
        for b in reads:
            b.last_r[self._key(op)] = op
        if dma_sem is not None:
            if dma_sem not in self.dma_cnt:
                self.dma_cnt[dma_sem] = 0
                self.dma_sems.append(dma_sem)
            self.dma_cnt[dma_sem] += 16
            op.val = self.dma_cnt[dma_sem]
        self.ops[eng].append(op)
        return op

    def region_last_use(self, region):
        m = -1
        for b in self.regions.get(region, []):
            if b.last_w is not None:
                m = max(m, b.last_w.idx)
            for r in b.last_r.values():
                m = max(m, r.idx)
        return m

    def finalize(self):
        for e in ENGS:
            c = 0
            for op in self.ops[e]:
                if op.dma_sem is None and op.needed:
                    c += 1
                    op.val = c
        for e in ENGS:
            for op in self.ops[e]:
                if op.dma_sem is not None and op.dma_mode == "group":
                    op.val = self.dma_cnt[op.dma_sem]

    def emit(self, eng_name, eng, sems):
        waited = {}
        for op in self.ops[eng_name]:
            for d in op.deps:
                sname = ("dma:" + d.dma_sem) if d.dma_sem else ("eng:" + d.eng)
                if waited.get(sname, 0) >= d.val:
                    continue
                eng.wait_ge(sems[sname], d.val)
                waited[sname] = d.val
            ins = op.fn(eng)
            if op.dma_sem is not None:
                ins.then_inc(sems["dma:" + op.dma_sem], 16)
            elif op.needed:
                ins.then_inc(sems["eng:" + eng_name], 1)


def build_program():
    nc = bass.Bass("TRN2", target_bir_lowering=False)
    P = Plan()

    def dram_in(name, shape):
        return nc.dram_tensor(name, list(shape), F32, kind="ExternalInput").ap()

    x_tm = dram_in("x_tm", [TOK, D])
    xT = dram_in("xT", [D, NH * 128])
    ctxT = dram_in("ctxT", [D, CTX])
    cvec = dram_in("cvec", [128, 32])
    w_ada = dram_in("w_ada", [D, 6 * D])
    b_ada = dram_in("b_ada", [1, 6 * D])
    b_adaT = dram_in("b_adaT", [128, 96])
    w_in = dram_in("w_in", [D, 3584])
    sink = dram_in("sink", [1, 8])
    gln_g = dram_in("gln_g", [1, 1024])
    gln_b = dram_in("gln_b", [1, 1024])
    w_sT = dram_in("w_sT", [128, 1024])
    b_s = dram_in("b_s", [1, 1024])
    w_out = dram_in("w_out", [D, D])
    ln1_g = dram_in("ln1_g", [1, D])
    ln1_b = dram_in("ln1_b", [1, D])
    ln1_gT = dram_in("ln1_gT", [128, 16])
    ln1_bT = dram_in("ln1_bT", [128, 16])
    w_ff1 = dram_in("w_ff1", [D, DFF])
    w_ff2 = dram_in("w_ff2", [DFF, D])
    ln2_g = dram_in("ln2_g", [1, D])
    ln2_b = dram_in("ln2_b", [1, D])
    ropeC = dram_in("ropeC", [NH * 128, 128])
    ropeS = dram_in("ropeS", [NH * 128, 128])
    masks = dram_in("masks", [128, 512])
    ident = dram_in("ident", [128, 128])
    sel4 = dram_in("sel4", [128, 2])
    y = nc.dram_tensor("y", [TOK, D], F32, kind="ExternalOutput").ap()
    dbg = {}

    es = contextlib.ExitStack()
    with es:
        def sb(name, nbytes):
            return es.enter_context(nc.sbuf_tensor(name, [128, nbytes // 4], F32))

        RA_ = sb("regA", 65536)
        RB_ = sb("regB", 16384)
        RC_ = sb("regC", 32768)
        RW_ = [sb("regW%d" % i, 16384) for i in range(4)]
        RD_ = sb("regD", 30720)
        PS_ = [es.enter_context(nc.psum_tensor("ps%d" % i, [128, 512], F32)) for i in range(8)]
        tiles = {"A": RA_, "B": RB_, "C": RC_, "D": RD_, "W0": RW_[0], "W1": RW_[1], "W2": RW_[2], "W3": RW_[3]}

        def mk(name, region, off, dt, shape):
            n = int(np.prod(shape))
            nbytes = n * (2 if dt == BF16 else 4)
            assert off % 4 == 0 and nbytes % 4 == 0
            assert off + nbytes <= tiles[region].shape[1] * 4, (name, region, off, nbytes)
            ap = tiles[region][:, off // 4:(off + nbytes) // 4]
            if dt == BF16:
                ap = ap.bitcast(BF16)
            if len(shape) == 2:
                ap = ap.rearrange("p (a b) -> p a b", a=shape[0], b=shape[1])
            elif len(shape) == 3:
                ap = ap.rearrange("p (a b c) -> p a b c", a=shape[0], b=shape[1], c=shape[2])
            return P.buf(name, region, off, nbytes, ap)

        PSB = [P.buf("ps%d" % i, "PS", i * 2048, 2048, PS_[i][:]) for i in range(8)]

        bank_reserved = set()

        def bank():
            while (P.bank_i % 8) in bank_reserved:
                P.bank_i += 1
            b = PSB[P.bank_i % 8]
            P.bank_i += 1
            return b

        HT = [mk("HT%d" % k, "A", k * 2560, BF16, [NH * 128]) for k in range(16)]
        KT = [mk("KT%d" % h, "A", 40960 + h * 2560, BF16, [NH * 128]) for h in range(2)]
        Vb = [mk("V%d" % t, "A", 46080 + t * 512, BF16, [256]) for t in range(NH)]
        KCT = mk("KCT", "A", 51200, BF16, [2, 256])
        VC = mk("VC", "A", 52224, BF16, [2, 256])
        ROPEC = mk("ROPEC", "A", 53248, F32, [NH, 128])
        ROPES = mk("ROPES", "A", 58368, F32, [NH, 128])
        X = [[mk("X%d_%d" % (t, n), "A", t * 8192 + n * 2048, F32, [512]) for n in range(4)] for t in range(NT)]
        Xall = mk("Xall", "A", 0, F32, [NT, 2048])
        P.regions["A"].remove(Xall)

        QT = [mk("QT%d" % h, "B", h * 2048, BF16, [TOK]) for h in range(8)]
        GUT = [mk("GUT%d" % h, "B", h * 2048, BF16, [TOK]) for h in range(8)]
        QTall = mk("QTall", "B", 0, BF16, [8, TOK]); P.regions["B"].remove(QTall)
        GATE1 = [mk("GATE1_%d" % n, "B", n * 2048, F32, [512]) for n in range(4)]
        ZB = [mk("ZB%d" % i, "B", 8192 + i * 4096, BF16, [2048]) for i in range(2)]
        HID = [[mk("HID%d_%d" % (k, g), "B", g * 8192 + k * 1024, BF16, [512]) for g in range(2)] for k in range(8)]

        XTR = [mk("XTR%d" % i, "C", i * 5120, F32, [NH * 128]) for i in range(3)]
        HCT = [mk("HCT%d" % k, "C", 15360 + k * 512, BF16, [CTX]) for k in range(16)]
        LB = mk("LB", "C", 23552, BF16, [16, 32])
        CTXT = mk("CTXT", "B", 0, F32, [16, CTX])
        CAT = [[mk("CAT%d_%d" % (k, t), "C", k * 2048 + t * 256, BF16, [128]) for t in range(NT)] for k in range(16)]
        H2T = [[mk("H2T%d_%d" % (k, t), "C", k * 2048 + t * 256, BF16, [128]) for t in range(NT)] for k in range(16)]
        CATall = mk("CATall", "C", 0, BF16, [16, TOK]); P.regions["C"].remove(CATall)
        H2Tall = CATall
        LN2G = mk("LN2G", "B", 8192, F32, [2048])

        def wslot(s, shape):
            return mk("W%d_%s" % (s, "x".join(map(str, shape))), "W%d" % s, 0, BF16, shape)
        WA = [wslot(s, [16, 512]) for s in range(4)]
        WF2 = [wslot(s, [4, 2048]) for s in range(4)]
        RA = [mk("RA%d" % i, "W3", i * 2048, F32, [512]) for i in range(2)]
        RBs = [mk("RB%d" % i, "W3", 4096 + i * 2048, F32, [512]) for i in range(2)]
        QR = [mk("QR%d" % i, "W3", 8192 + i * 1024, BF16, [512]) for i in range(3)]
        PT = [mk("PT%d" % i, "W3", i * 5120, BF16, [5, 512]) for i in range(3)]
        GG = [mk("GG%d" % i, "W3", i * 4096, F32, [1024]) for i in range(2)]
        GNB = [mk("GNB%d" % i, "W3", 8192 + i * 2048, BF16, [1024]) for i in range(2)]
        TMPG = mk("TMPG", "W3", 12288, F32, [512])
        LN1G = mk("LN1G", "W3", 0, F32, [2048])
        LN1B = mk("LN1B", "W3", 8192, F32, [2048])

        o = 0
        def dmk(name, dt, shape):
            nonlocal o
            n = int(np.prod(shape)) * (2 if dt == BF16 else 4)
            n = (n + 31) // 32 * 32
            b = mk(name, "D", o, dt, shape)
            o += n
            return b
        BSB = dmk("BSB", F32, [1024])
        GLNG = dmk("GLNG", F32, [1024])
        GLNB = dmk("GLNB", F32, [1024])
        ffn_tmp_end = o
        WST = dmk("WST", BF16, [8, 128])
        MASK = dmk("MASK", BF16, [4, 128])
        IDB = dmk("IDB", BF16, [128])
        IDF = dmk("IDF", F32, [128])
        ONESB = dmk("ONESB", BF16, [128])
        ONESF = dmk("ONESF", F32, [128])
        DG = [dmk("DG%d" % i, F32, [128]) for i in range(2)]
        LF = dmk("LF", BF16, [16, 32])
        SEL4 = dmk("SEL4", F32, [2])
        MT = [dmk("MT%d" % i, F32, [512]) for i in range(2)]
        CV = dmk("CV", F32, [32])
        SV = dmk("SV", BF16, [16, 2])
        BADAT = dmk("BADAT", F32, [96])
        MODCOL = [dmk("MODCOL%d" % i, F32, [16, 2]) for i in range(6)]
        LN1GT = dmk("LN1GT", F32, [16])
        LN1BT = dmk("LN1BT", F32, [16])
        A2 = dmk("A2", F32, [16])
        B2 = dmk("B2", F32, [16])
        SINK8 = dmk("SINK8", F32, [8])
        EXPS = dmk("EXPS", F32, [8])
        EXPSB = mk("EXPSB", "A", 63488, BF16, [1024])
        RDEN = dmk("RDEN", F32, [512])
        ST = [dmk("ST%d" % i, F32, [24]) for i in range(2)]
        MV = [dmk("MV%d" % i, F32, [8]) for i in range(8)]
        GATE2 = [mk("GATE2_%d" % n, "D", n * 2048, F32, [512]) for n in range(4)]
        RELU = [mk("RELU%d" % i, "D", 8192 + i * 2048, F32, [512]) for i in range(2)]
        TMP = [mk("TMP%d" % i, "D", 8192 + i * 2048, F32, [512]) for i in range(2)]
        LN2B = mk("LN2B", "D", WST.lo, F32, [2048])
        assert WST.lo + 8192 <= MT[1].hi
        assert 8192 + 4096 <= ffn_tmp_end

        def dma(queue, out_ap, in_ap, sem, mode, reads=(), writes=()):
            def fn(e):
                return e.dma_start(out=out_ap, in_=in_ap)
            return P.add(queue, fn, reads, writes, dma_sem=sem, dma_mode=mode)

        def act(out_ap, in_ap, func, reads, writes, bias=None, scale=None):
            kw = {}
            if bias is not None:
                kw["bias"] = bias
            if scale is not None:
                kw["scale"] = scale
            def fn(e):
                return e.activation(out=out_ap, in_=in_ap, func=func, **kw)
            return P.add("scalar", fn, reads, writes)

        def vec(fn, reads, writes):
            return P.add("vector", fn, reads, writes)

        def mm_group(ps_ap, pairs, reads, writes):
            def fn(te):
                n = len(pairs)
                ins = None
                for i, (l, r) in enumerate(pairs):
                    ins = te.matmul(ps_ap, lhsT=l, rhs=r, start=(i == 0), stop=(i == n - 1))
                return ins
            return P.add("tensor", fn, reads, writes)

        def transposes(items, reads, writes):
            def fn(te):
                ins = None
                for (o_, i_) in items:
                    ins = te.transpose(o_, i_, IDB.ap)
                return ins
            return P.add("tensor", fn, reads + [IDB], writes)

        def pipeline(n, stage1, stage2, depth=1, hook=None):
            pend = []
            for i in range(n):
                ctx_ = stage1(i)
                pend.append((i, ctx_))
                if len(pend) > depth:
                    j, c_ = pend.pop(0)
                    stage2(j, c_)
                if hook is not None:
                    hook(i)
            for j, c_ in pend:
                stage2(j, c_)

        wstate = {"i": 0, "ffn": False, "hold3": True}

        wpinned = set()

        def wfree(*bufs):
            for b in bufs:
                wpinned.discard(int(b.region[1]))

        def wload(kind, idx):
            if wstate["ffn"]:
                allowed = (0, 1, 2) if wstate["hold3"] else (0, 1, 2, 3)
            else:
                allowed = (0, 1, 2, 3) if wstate["i"] < 9 else (0, 1, 2)
            allowed = [q for q in allowed if q not in wpinned]
            assert allowed, "no free weight slot"
            s = min(allowed, key=lambda q: P.region_last_use("W%d" % q))
            wpinned.add(s)
            wstate["i"] += 1
            if kind == "ada":
                src = w_ada.rearrange("(k p) n -> p k n", p=128)[:, :, idx * 512:(idx + 1) * 512]
                dst = WA[s]
            elif kind == "win":
                src = w_in.rearrange("(k p) n -> p k n", p=128)[:, :, idx * 512:(idx + 1) * 512]
                dst = WA[s]
            elif kind == "wout":
                src = w_out.rearrange("(k p) n -> p k n", p=128)[:, :, idx * 512:(idx + 1) * 512]
                dst = WA[s]
            elif kind == "ff1":
                src = w_ff1.rearrange("(k p) n -> p k n", p=128)[:, :, idx * 512:(idx + 1) * 512]
                dst = WA[s]
            else:
                src = w_ff2.rearrange("(k p) n -> p k n", p=128)[:, idx * 4:(idx + 1) * 4, :]
                dst = WF2[s]
            dma("gpsimd", dst.ap, src, "w%d" % s, "slot", writes=[dst])
            return dst

        def cload(buf, src, queue="sync", grp="a"):
            dma(queue, buf.ap, src, "cst%s_%s" % (grp, queue), "group", writes=[buf])
        cload(CV, cvec)
        cload(IDF, ident)
        cload(SEL4, sel4)
        cload(BADAT, b_adaT)
        cload(CTXT, ctxT.rearrange("(k p) t -> p k t", p=128))

        def late_consts():
            cload(SINK8, sink.partition_broadcast(128), grp="b")
            cload(ROPEC, ropeC.rearrange("(t p) d -> p t d", p=128), grp="b")
            cload(ROPES, ropeS.rearrange("(t p) d -> p t d", p=128), grp="b")
            cload(BSB, b_s.partition_broadcast(128), grp="b")
            cload(GLNG, gln_g.partition_broadcast(128), grp="b")
            cload(GLNB, gln_b.partition_broadcast(128), grp="b")
            cload(LN1GT, ln1_gT, grp="b")
            cload(LN1BT, ln1_bT, grp="b")
        cload(IDB, ident, "gpsimd")
        cload(MASK, masks.rearrange("p (a b) -> p a b", a=4), "gpsimd")
        cload(WST, w_sT.rearrange("p (a b) -> p a b", a=8), "gpsimd")
        vec(lambda e: e.memset(ONESB.ap, 1.0), [], [ONESB])
        vec(lambda e: e.memset(ONESF.ap, 1.0), [], [ONESF])

        act(SV.ap, CV.ap.rearrange("p (a b) -> p a b", a=16), AF.Silu, [CV], [SV])
        vec(lambda e: e.tensor_copy(out=LB.ap[:, :, 0:16], in_=SV.ap[:, :, 0:1].to_broadcast([128, 16, 16])), [SV], [LB])
        vec(lambda e: e.tensor_copy(out=LB.ap[:, :, 16:32], in_=SV.ap[:, :, 1:2].to_broadcast([128, 16, 16])), [SV], [LB])
        vec(lambda e: e.tensor_copy(out=LF.ap, in_=SV.ap[:, :, 0:1].to_broadcast([128, 16, 32])), [SV], [LF])

        ada_pend = []
        ada_cnt = [0]

        def ada_stage2(n, pb, mt):
            chunk, q = n // 4, n % 4
            pb2 = bank()
            def fn(te):
                ins = None
                for i in range(4):
                    ins = te.matmul(pb2.ap[:, 2 * i:2 * i + 2], lhsT=mt.ap[:, i * 128:(i + 1) * 128],
                                    rhs=SEL4.ap, start=True, stop=True)
                return ins
            P.add("tensor", fn, [mt, SEL4], [pb2])
            mc = MODCOL[chunk]
            def fn2(e):
                return e.tensor_tensor(
                    out=mc.ap[:, q * 4:q * 4 + 4, :],
                    in0=pb2.ap[:, 0:8].rearrange("p (a b) -> p a b", a=4),
                    in1=BADAT.ap[:, chunk * 16 + q * 4:chunk * 16 + q * 4 + 4].unsqueeze(2).to_broadcast([128, 4, 2]),
                    op=ALU.add)
            vec(fn2, [pb2, BADAT], [mc])

        def ada_flush():
            while ada_pend:
                ada_stage2(*ada_pend.pop(0))

        def ada_chunk(n):
            W = wload("ada", n)
            lhs = LB if n < 8 else LF
            pb = bank()
            mt = MT[ada_cnt[0] % 2]
            ada_cnt[0] += 1
            def fng(te):
                ins = None
                for i in range(4):
                    for g in range(4):
                        k = 4 * g + i
                        ins = te.matmul(pb.ap[32 * g:32 * g + 32, :], lhsT=lhs.ap[:, k, :], rhs=W.ap[:, k, :],
                                        start=(i == 0), stop=(i == 3), tile_position=(0, 32 * g))
                return ins
            P.add("tensor", fng, [lhs, W], [pb])
            wfree(W)
            vec(lambda e: e.tensor_copy(out=mt.ap, in_=pb.ap), [pb], [mt])
            ada_flush()
            ada_pend.append((n, pb, mt))

        for k in range(16):
            xs = XTR[k % 3]
        xt_issued = [0]

        def issue_xt(k):
            xs = XTR[k % 3]
            dma("sync", xs.ap, xT[k * 128:(k + 1) * 128, :], "xt%d" % (k % 3), "slot", writes=[xs])

        for q in range(4):
            ada_chunk(q)
            ada_chunk(4 + q)
            ada_flush()
            sl = slice(4 * q, 4 * q + 4)
            vec(lambda e, sl=sl: e.tensor_scalar(out=MODCOL[1].ap[:, sl, :], in0=MODCOL[1].ap[:, sl, :], scalar1=1.0, scalar2=None, op0=ALU.add),
                [MODCOL[1]], [MODCOL[1]])
            for k in range(4 * q, 4 * q + 4):
                act(HCT[k].ap, CTXT.ap[:, k, :], AF.Identity, [CTXT, MODCOL[0], MODCOL[1]], [HCT[k]],
                    bias=MODCOL[0].ap[:, k, 1:2], scale=MODCOL[1].ap[:, k, 1:2])
            for k in range(4 * q, 4 * q + 4):
                xs = XTR[k % 3]
                issue_xt(k)
                act(HT[k].ap, xs.ap, AF.Identity, [xs, MODCOL[0], MODCOL[1]], [HT[k]],
                    bias=MODCOL[0].ap[:, k, 0:1], scale=MODCOL[1].ap[:, k, 0:1])
            if q == 2:
                late_consts()

        rope_i = [0]

        def rope(pb, col0, nh, tpos, qr):
            i = rope_i[0] % 2
            rope_i[0] += 1
            ra, rb = RA[i], RBs[i]
            n = nh * 128
            src = pb.ap[:, col0:col0 + n]
            src3 = src.rearrange("p (h d) -> p h d", h=nh)
            src5 = src.rearrange("p (h f j d) -> p h f j d", h=nh, f=2, j=2)
            rb5 = rb.ap[:, 0:n].rearrange("p (h f j d) -> p h f j d", h=nh, f=2, j=2)
            s4 = ROPES.ap[:, tpos, :].rearrange("p (f j d) -> p f j d", f=2, j=2)
            cb = ROPEC.ap[:, tpos, :].unsqueeze(1).to_broadcast([128, nh, 128])
            vec(lambda e: e.tensor_tensor(out=ra.ap[:, 0:n].rearrange("p (h d) -> p h d", h=nh), in0=src3, in1=cb, op=ALU.mult),
                [pb, ROPEC], [ra])
            vec(lambda e: e.tensor_tensor(out=rb5[:, :, :, 0, :], in0=src5[:, :, :, 1, :],
                                          in1=s4[:, :, 0, :].unsqueeze(1).to_broadcast([128, nh, 2, 32]), op=ALU.mult),
                [pb, ROPES], [rb])
            vec(lambda e: e.tensor_tensor(out=rb5[:, :, :, 1, :], in0=src5[:, :, :, 0, :],
                                          in1=s4[:, :, 1, :].unsqueeze(1).to_broadcast([128, nh, 2, 32]), op=ALU.mult),
                [pb, ROPES], [rb])
            vec(lambda e: e.tensor_tensor(out=qr.ap[:, 0:n], in0=ra.ap[:, 0:n], in1=rb.ap[:, 0:n], op=ALU.add),
                [ra, rb], [qr])

        W = wload("win", 2)
        for hk in range(2):
            pb = bank()
            mm_group(pb.ap[:, 0:256], [(W.ap[:, k, hk * 128:(hk + 1) * 128], HCT[k].ap) for k in range(16)], [W] + HCT, [pb])
            act(KCT.ap[:, hk, :], pb.ap[:, 0:256], AF.Copy, [pb], [KCT])
        for tb in range(2):
            pb = bank()
            mm_group(pb.ap[:, 0:256], [(HCT[k].ap[:, tb * 128:(tb + 1) * 128], W.ap[:, k, 256:512]) for k in range(16)], [W] + HCT, [pb])
            act(VC.ap[:, tb, :], pb.ap[:, 0:256], AF.Copy, [pb], [VC])

        def kv_s1(t, W=W):
            pb = bank()
            mm_group(pb.ap, [(HT[k].ap[:, t * 128:(t + 1) * 128], W.ap[:, k, :]) for k in range(16)], [W] + HT, [pb])
            qr = QR[t % 3]
            rope(pb, 0, 2, t, qr)
            vec(lambda e: e.tensor_copy(out=Vb[t].ap, in_=pb.ap[:, 256:512]), [pb], [Vb[t]])
            return qr

        def kv_s2(t, qr):
            pt_ = bank()
            ptb = pt_.ap.bitcast(BF16)
            transposes([(ptb[:, h * 128:(h + 1) * 128], qr.ap[:, h * 128:(h + 1) * 128]) for h in range(2)], [qr], [pt_])
            for h in range(2):
                act(KT[h].ap[:, t * 128:(t + 1) * 128], ptb[:, h * 128:(h + 1) * 128], AF.Copy, [pt_], [KT[h]])
        pipeline(NH, kv_s1, kv_s2, depth=2)
        wfree(W)
        ada_chunk(8)
        ada_chunk(9)

        for c in range(2):
            W = wload("win", c)

            def q_s1(t, W=W):
                pb = bank()
                mm_group(pb.ap, [(HT[k].ap[:, (t + 1) * 128:(t + 2) * 128], W.ap[:, k, :]) for k in range(16)], [W] + HT, [pb])
                qr = QR[t % 3]
                rope(pb, 0, 4, t + 1, qr)
                return qr

            def q_s2(t, qr, c=c):
                pt_ = bank()
                ptb = pt_.ap.bitcast(BF16)
                transposes([(ptb[:, h * 128:(h + 1) * 128], qr.ap[:, h * 128:(h + 1) * 128]) for h in range(4)], [qr], [pt_])
                act(QTall.ap[:, 4 * c:4 * c + 4, t * 128:(t + 1) * 128], ptb[:, 0:512].rearrange("p (h q) -> p h q", h=4), AF.Copy,
                    [pt_], QT[4 * c:4 * c + 4])
            pipeline(NT, q_s1, q_s2, depth=2)
            wfree(W)
            ada_chunk(10 + 2 * c)
            ada_chunk(11 + 2 * c)

        act(EXPS.ap, SINK8.ap, AF.Exp, [SINK8], [EXPS])
        vec(lambda e: e.tensor_copy(out=EXPSB.ap[0:1, :].rearrange("p (h q) -> p h q", h=8),
                                    in_=EXPS.ap[0:1, :].unsqueeze(2).to_broadcast([1, 8, 128])), [EXPS], [EXPSB])

        def at_s1(it):
            t, hk = it // 2, it % 2
            pt = PT[it % 3]
            qheads = QT[4 * hk:4 * hk + 4]
            rhs_q = QTall.ap[:, 4 * hk:4 * hk + 4, t * 128:(t + 1) * 128]
            sb_ = []
            for j in range(5):
                pb = bank()
                if j < 3:
                    kk = KT[hk].ap[:, (t + j) * 128:(t + j + 1) * 128]
                    rd = [KT[hk]]
                else:
                    kk = KCT.ap[:, hk, (j - 3) * 128:(j - 2) * 128]
                    rd = [KCT]
                mm_group(pb.ap.rearrange("p (h q) -> p h q", h=4), [(kk, rhs_q)], rd + qheads, [pb])
                sb_.append(pb)
            for j in range(5):
                act(pt.ap[:, j, :], sb_[j].ap, AF.Exp, [sb_[j]], [pt], scale=QSCALE)
            mL = 2 if t == 0 else 0
            mU = 3 if t == NT - 1 else 1
            for (j, mi) in ((0, mL), (2, mU)):
                def fnm(e, j=j, mi=mi, pt=pt):
                    v3 = pt.ap[:, j, :].rearrange("p (h q) -> p h q", h=4)
                    return e.tensor_tensor(out=v3, in0=v3, in1=MASK.ap[:, mi, :].unsqueeze(1).to_broadcast([128, 4, 128]), op=ALU.mult)
                vec(fnm, [pt, MASK], [pt])
            return pt

        def at_s2(it, pt):
            t, hk = it // 2, it % 2
            po = bank()
            pd = bank()
            pairs = []
            rds = [pt, VC]
            for j in range(5):
                if j < 3:
                    vv = Vb[t + j].ap[:, hk * 128:(hk + 1) * 128]
                    rds.append(Vb[t + j])
                else:
                    vv = VC.ap[:, j - 3, hk * 128:(hk + 1) * 128]
                pairs.append((vv, pt.ap[:, j, :]))
            mm_group(po.ap, pairs, rds, [po])
            mm_group(pd.ap, [(ONESB.ap, pt.ap[:, j, :]) for j in range(5)] + [(ONESB.ap[0:1, :], EXPSB.ap[0:1, hk * 512:(hk + 1) * 512])],
                     [pt, ONESB, EXPSB], [pd])
            vec(lambda e: e.reciprocal(out=RDEN.ap, in_=pd.ap), [pd], [RDEN])
            def fno(e):
                return e.tensor_tensor(out=CATall.ap[:, 4 * hk:4 * hk + 4, t * 128:(t + 1) * 128],
                                       in0=po.ap.rearrange("p (h q) -> p h q", h=4),
                                       in1=RDEN.ap.rearrange("p (h q) -> p h q", h=4), op=ALU.mult)
            vec(fno, [po, RDEN], [CAT[4 * hk + h][t] for h in range(4)])

        at_ada = {1: 14, 4: 15, 7: 16, 10: 17, 12: 18, 14: 19}

        def at_hook(it):
            if it in at_ada:
                ada_chunk(at_ada[it])
        pipeline(2 * NT, at_s1, at_s2, depth=2, hook=at_hook)

        for c in (3, 4):
            W = wload("win", c)
            for m in range(4):
                h = (c - 3) * 4 + m
                for tg in range(2):
                    pb = bank()
                    mm_group(pb.ap, [(W.ap[:, k, m * 128:(m + 1) * 128], HT[k].ap[:, 128 + tg * 512:128 + (tg + 1) * 512]) for k in range(16)],
                             [W] + HT, [pb])
                    act(GUT[h].ap[:, tg * 512:(tg + 1) * 512], pb.ap, AF.Gelu_apprx_tanh, [pb], [GUT[h]])
            wfree(W)
        ada_flush()

        Wg = [wload("win", 5), wload("win", 6)]
        GUTall = QTall

        def g_s1a(t):
            gg = GG[t % 2]
            st, mv = ST[t % 2], MV[t % 8]
            pbs = []
            for half in range(2):
                pb = bank()
                mm_group(pb.ap, [(HT[k].ap[:, (t + 1) * 128:(t + 2) * 128], Wg[half].ap[:, k, :]) for k in range(16)], [Wg[half]] + HT, [pb])
                pbs.append(pb)
            for half in range(2):
                act(gg.ap[:, half * 512:(half + 1) * 512], pbs[half].ap, AF.Gelu_apprx_tanh, [pbs[half]], [gg])
            for half in range(2):
                vec(lambda e, half=half: e.bn_stats(out=st.ap[:, half * 6:(half + 1) * 6], in_=gg.ap[:, half * 512:(half + 1) * 512]), [gg], [st])
            vec(lambda e: e.bn_aggr(out=mv.ap[:, 0:2], in_=st.ap[:, 0:12]), [st], [mv])
            vec(lambda e: e.tensor_scalar(out=mv.ap[:, 2:3], in0=mv.ap[:, 1:2], scalar1=EPS, scalar2=None, op0=ALU.add), [mv], [mv])
            act(mv.ap[:, 3:4], mv.ap[:, 2:3], AF.Sqrt, [mv], [mv])

        def g_s1b(t):
            gg = GG[t % 2]
            gnb = GNB[t % 2]
            mv = MV[t % 8]
            vec(lambda e: e.reciprocal(out=mv.ap[:, 4:5], in_=mv.ap[:, 3:4]), [mv], [mv])
            vec(lambda e: e.scalar_tensor_tensor(out=gg.ap, in0=gg.ap, scalar=mv.ap[:, 0:1], in1=GLNG.ap, op0=ALU.subtract, op1=ALU.mult),
                [gg, mv, GLNG], [gg])
            vec(lambda e: e.scalar_tensor_tensor(out=gnb.ap, in0=gg.ap, scalar=mv.ap[:, 4:5], in1=GLNB.ap, op0=ALU.mult, op1=ALU.add),
                [gg, mv, GLNB], [gnb])
            return gnb

        def g_s2(t, gnb):
            for i in range(2):
                pb = bank()
                def fnm(te, pb=pb, i=i):
                    ins = None
                    for hh in range(4):
                        h = 4 * i + hh
                        ins = te.matmul(pb.ap[:, hh * 128:(hh + 1) * 128], lhsT=gnb.ap[:, h * 128:(h + 1) * 128], rhs=WST.ap[:, h, :], start=True, stop=True)
                    return ins
                P.add("tensor", fnm, [gnb, WST], [pb])
                vec(lambda e, pb=pb, i=i: e.tensor_tensor(out=TMPG.ap, in0=pb.ap, in1=BSB.ap[:, i * 512:(i + 1) * 512], op=ALU.add), [pb, BSB], [TMPG])
                def fnc(e, i=i):
                    return e.tensor_tensor(out=CATall.ap[:, 8 + 4 * i:12 + 4 * i, t * 128:(t + 1) * 128],
                                           in0=TMPG.ap.rearrange("p (h q) -> p h q", h=4),
                                           in1=GUTall.ap[:, 4 * i:4 * i + 4, t * 128:(t + 1) * 128], op=ALU.mult)
                vec(fnc, [TMPG] + GUT[4 * i:4 * i + 4], [CAT[8 + 4 * i + h][t] for h in range(4)])
        wout_pend = []
        gate1_pending = [True]

        def wout_group(W, n, t):
            pb = bank()
            tmp = TMP[(t + n) % 2]
            mm_group(pb.ap, [(CATall.ap[:, k, t * 128:(t + 1) * 128], W.ap[:, k, :]) for k in range(16)],
                     [W] + [CAT[k][t] for k in range(16)], [pb])
            xb = X[t][n]
            bi = PSB.index(pb)

            def evac():
                bank_reserved.discard(bi)
                vec(lambda e: e.tensor_tensor(out=tmp.ap, in0=pb.ap, in1=GATE1[n].ap, op=ALU.mult), [pb, GATE1[n]], [tmp])
                vec(lambda e: e.scalar_tensor_tensor(out=xb.ap, in0=xb.ap, scalar=ALPHA, in1=tmp.ap, op0=ALU.mult, op1=ALU.add), [xb, tmp], [xb])
            if gate1_pending[0]:
                bank_reserved.add(bi)
                wout_pend.append(evac)
            else:
                evac()

        Wo0 = [None]
        for t in range(NT + 2):
            if t < NT:
                g_s1a(t)
            if 1 <= t <= NT:
                g_s1b(t - 1)
            if t == NT:
                Wo0[0] = wload("wout", 0)
                wout_group(Wo0[0], 0, 0)
                wout_group(Wo0[0], 0, 1)
            if t == NT + 1:
                wout_group(Wo0[0], 0, 2)
                wout_group(Wo0[0], 0, 3)
            if t >= 2:
                g_s2(t - 2, GNB[(t - 2) % 2])
            if t in (2, 5):
                ada_chunk(20 + (t - 2) // 3)
        wfree(*Wg)

        vec(lambda e: e.tensor_scalar(out=MODCOL[4].ap, in0=MODCOL[4].ap, scalar1=1.0, scalar2=None, op0=ALU.add), [MODCOL[4]], [MODCOL[4]])
        vec(lambda e: e.tensor_tensor(out=A2.ap, in0=LN1GT.ap, in1=MODCOL[4].ap[:, :, 0], op=ALU.mult), [LN1GT, MODCOL[4]], [A2])
        vec(lambda e: e.tensor_tensor(out=B2.ap, in0=LN1BT.ap, in1=MODCOL[4].ap[:, :, 0], op=ALU.mult), [LN1BT, MODCOL[4]], [B2])
        vec(lambda e: e.tensor_tensor(out=B2.ap, in0=B2.ap, in1=MODCOL[3].ap[:, :, 0], op=ALU.add), [B2, MODCOL[3]], [B2])

        def gate_bcast(mc, G):
            for n in range(4):
                pb = bank()
                for i in range(4):
                    kc = 4 * n + i
                    dg = DG[kc % 2]
                    vec(lambda e, dg=dg, kc=kc: e.tensor_scalar(out=dg.ap, in0=IDF.ap, scalar1=mc.ap[:, kc, 0:1], scalar2=None, op0=ALU.mult),
                        [IDF, mc], [dg])
                    def fn(te, pb=pb, i=i, dg=dg):
                        return te.matmul(pb.ap[:, i * 128:(i + 1) * 128], lhsT=ONESF.ap, rhs=dg.ap, start=True, stop=True)
                    P.add("tensor", fn, [ONESF, dg], [pb])
                vec(lambda e, pb=pb, n=n: e.tensor_copy(out=G[n].ap, in_=pb.ap), [pb], [G[n]])

        Xrow = lambda t: Xall.ap[:, t, :]

        def ln_stats_a(t):
            st, mv = ST[t % 2], MV[t % 8]
            for c in range(4):
                vec(lambda e, c=c: e.bn_stats(out=st.ap[:, c * 6:(c + 1) * 6], in_=X[t][c].ap), [X[t][c]], [st])
            vec(lambda e: e.bn_aggr(out=mv.ap[:, 0:2], in_=st.ap), [st], [mv])
            vec(lambda e: e.tensor_scalar(out=mv.ap[:, 2:3], in0=mv.ap[:, 1:2], scalar1=EPS, scalar2=None, op0=ALU.add), [mv], [mv])
            act(mv.ap[:, 3:4], mv.ap[:, 2:3], AF.Sqrt, [mv], [mv])
            return mv

        def ln_stats_b(t):
            mv = MV[t % 8]
            vec(lambda e: e.reciprocal(out=mv.ap[:, 4:5], in_=mv.ap[:, 3:4]), [mv], [mv])
            return mv

        def ln_stats(t):
            ln_stats_a(t)
            return ln_stats_b(t)

        def ln_apply(t, mv, G, Bb, eng="vector"):
            if eng == "vector":
                vec(lambda e: e.scalar_tensor_tensor(out=Xrow(t), in0=Xrow(t), scalar=mv.ap[:, 0:1], in1=G.ap, op0=ALU.subtract, op1=ALU.mult),
                    X[t] + [mv, G], X[t])
                vec(lambda e: e.scalar_tensor_tensor(out=Xrow(t), in0=Xrow(t), scalar=mv.ap[:, 4:5], in1=Bb.ap, op0=ALU.mult, op1=ALU.add),
                    X[t] + [mv, Bb], X[t])
            else:
                P.add(eng, lambda e: e.tensor_scalar(out=Xrow(t), in0=Xrow(t), scalar1=mv.ap[:, 0:1], scalar2=mv.ap[:, 4:5], op0=ALU.subtract, op1=ALU.mult),
                      X[t] + [mv], X[t])
                P.add(eng, lambda e: e.tensor_tensor(out=Xrow(t), in0=Xrow(t), in1=G.ap, op=ALU.mult), X[t] + [G], X[t])
                P.add(eng, lambda e: e.tensor_tensor(out=Xrow(t), in0=Xrow(t), in1=Bb.ap, op=ALU.add), X[t] + [Bb], X[t])

        def ln1_x1(t):
            ln_apply(t, MV[t % 8], LN1G, LN1B)

        def ln1_zb(t):
            mv = MV[t % 8]
            zb = ZB[t % 2]
            vec(lambda e: e.tensor_scalar(out=zb.ap, in0=Xrow(t), scalar1=mv.ap[:, 0:1], scalar2=mv.ap[:, 4:5], op0=ALU.subtract, op1=ALU.mult),
                X[t] + [mv], [zb])

        def ln1_tr(t):
            zb = ZB[t % 2]
            for i in range(2):
                pt_ = bank()
                ptb = pt_.ap.bitcast(BF16)
                transposes([(ptb[:, j * 128:(j + 1) * 128], zb.ap[:, (8 * i + j) * 128:(8 * i + j + 1) * 128]) for j in range(8)], [zb], [pt_])
                for j in range(8):
                    k = 8 * i + j
                    act(H2Tall.ap[:, k, t * 128:(t + 1) * 128], ptb[:, j * 128:(j + 1) * 128], AF.Identity, [pt_, A2, B2], [H2T[k][t]],
                        bias=B2.ap[:, k:k + 1], scale=A2.ap[:, k:k + 1])

        for n in range(4):
            wr = [X[t][n] for t in range(NT)]
            dma("sync", Xall.ap[:, :, n * 512:(n + 1) * 512], x_tm.rearrange("(t p) c -> p t c", p=128)[:, :, n * 512:(n + 1) * 512],
                "x%d" % n, "slot", writes=wr)
        dma("sync", LN1G.ap, ln1_g.partition_broadcast(128), "ln1p", "group", writes=[LN1G])
        dma("sync", LN1B.ap, ln1_b.partition_broadcast(128), "ln1p", "group", writes=[LN1B])
        for n in range(3):
            W = Wo0[0] if n == 0 else wload("wout", n)
            for t in range(NT):
                if n == 0 and t < 4:
                    continue
                if n == 0 and t == 4:
                    pend_ev = wout_pend[:]
                    del wout_pend[:]
                    gate_bcast(MODCOL[2], GATE1)
                    for f in pend_ev:
                        f()
                    gate1_pending[0] = False
                wout_group(W, n, t)
            wfree(W)

        W = wload("wout", 3)
        for t in range(NT):
            wout_group(W, 3, t)
            ln_stats_a(t)
            if t >= 1:
                ln_stats_b(t - 1)
                ln1_zb(t - 1)
            if t >= 2:
                ln1_tr(t - 2)
        wfree(W)
        ln_stats_b(NT - 1)
        ln1_zb(NT - 1)
        ln1_tr(NT - 2)

        wstate["ffn"] = True
        wstate["i"] = 0
        ln2_loaded = [False]
        relu_i = [0]
        def ffn1_group(W, jj, m, tg, hg=None, blocks=None, sq_on_act=False):
            b0, nb = (4 * tg, 4) if blocks is None else blocks
            ncol = nb * 128
            hk_ = HID[jj * 4 + m][tg if hg is None else hg]
            pb = bank()
            mm_group(pb.ap[:, 0:ncol], [(W.ap[:, k, m * 128:(m + 1) * 128], H2Tall.ap[:, k, b0 * 128:b0 * 128 + ncol]) for k in range(16)],
                     [W] + [H2T[k][t] for k in range(16) for t in range(b0, b0 + nb)], [pb])
            rl = RELU[relu_i[0] % 2]
            relu_i[0] += 1
            act(rl.ap[:, 0:ncol], pb.ap[:, 0:ncol], AF.Relu, [pb], [rl])
            if sq_on_act:
                act(hk_.ap[:, 0:ncol], rl.ap[:, 0:ncol], AF.Square, [rl], [hk_])
            else:
                vec(lambda e: e.tensor_tensor(out=hk_.ap[:, 0:ncol], in0=rl.ap[:, 0:ncol], in1=rl.ap[:, 0:ncol], op=ALU.mult), [rl], [hk_])

        def ffn2_block(p, t, W2, hg=None, tt=None):
            g = t // 4 if hg is None else hg
            tt = t % 4 if tt is None else tt
            for n in range(4):
                pb = bank()
                tmp = TMP[n % 2]
                mm_group(pb.ap, [(HID[kc][g].ap[:, tt * 128:(tt + 1) * 128], W2[kc // 4].ap[:, kc % 4, n * 512:(n + 1) * 512]) for kc in range(8)],
                         W2 + [HID[kc][g] for kc in range(8)], [pb])
                vec(lambda e, pb=pb, n=n, tmp=tmp: e.tensor_tensor(out=tmp.ap, in0=pb.ap, in1=GATE2[n].ap, op=ALU.mult), [pb, GATE2[n]], [tmp])
                xb = X[t][n]
                if p == 0:
                    vec(lambda e, xb=xb, tmp=tmp: e.scalar_tensor_tensor(out=xb.ap, in0=xb.ap, scalar=ALPHA, in1=tmp.ap, op0=ALU.mult, op1=ALU.add), [xb, tmp], [xb])
                else:
                    vec(lambda e, xb=xb, tmp=tmp: e.tensor_tensor(out=xb.ap, in0=xb.ap, in1=tmp.ap, op=ALU.add), [xb, tmp], [xb])

        def ln2_finish(t):
            mv = ln_stats_b(t)
            ln_apply(t, mv, LN2G, LN2B)
            dma("sync", y[t * 128:(t + 1) * 128, :], Xrow(t), "out", "group", reads=X[t])

        for p in range(8):
            if p == 0:
                W1 = [wload("ff1", 0), wload("ff1", 1)]
                gi = 0
                for tg in range(2):
                    for jj in range(2):
                        for m in range(4):
                            if gi == 0:
                                ln1_tr(NT - 1)
                            ffn1_group(W1[jj], jj, m, tg, sq_on_act=True)
                            if gi < NT:
                                ln1_x1(gi)
                            elif gi in (9, 14):
                                ada_chunk(22 + (gi - 9) // 5)
                            gi += 1
                wfree(*W1)
                wstate["hold3"] = False
                ada_flush()
                gate_bcast(MODCOL[5], GATE2)
                W2 = [wload("ff2", 0), wload("ff2", 1)]
                for t in range(NT):
                    ffn2_block(p, t, W2)
                wfree(*W2)
            elif p < 7:
                for jj in range(2):
                    W = wload("ff1", 2 * p + jj)
                    for m in range(4):
                        for tg in range(2):
                            ffn1_group(W, jj, m, tg)
                    wfree(W)
                W2 = [wload("ff2", 2 * p), wload("ff2", 2 * p + 1)]
                for t in range(NT):
                    ffn2_block(p, t, W2)
                wfree(*W2)
            else:
                W1 = [wload("ff1", 2 * p), wload("ff1", 2 * p + 1)]
                W2 = [wload("ff2", 2 * p), wload("ff2", 2 * p + 1)]
                dma("sync", LN2G.ap, ln2_g.partition_broadcast(128), "ln2p", "group", writes=[LN2G])
                dma("sync", LN2B.ap, ln2_b.partition_broadcast(128), "ln2p", "group", writes=[LN2B])
                for (b0, nb) in ((0, 4), (4, 4)):
                    for jj in range(2):
                        for m in range(4):
                            ffn1_group(W1[jj], jj, m, None, hg=0, blocks=(b0, nb))
                    for t in range(b0, b0 + nb):
                        ffn2_block(p, t, W2, hg=0, tt=t - b0)
                        ln_stats_a(t)
                        if t >= 1:
                            ln2_finish(t - 1)
                ln2_finish(NT - 1)

        P.finalize()
        sem_names = ["eng:" + e for e in ENGS] + ["dma:" + s for s in P.dma_sems]
        sems = {}
        for sn in sem_names:
            sems[sn] = es.enter_context(nc.semaphore(sn.replace(":", "_")))
        block = es.enter_context(nc.Block())
        final_waits = [("dma:out", P.dma_cnt["out"])]
        if DEBUG and "dbg" in P.dma_cnt:
            final_waits.append(("dma:dbg", P.dma_cnt["dbg"]))

        with nc.allow_low_precision("bf16 matmuls, fp32 accumulation"):
            @block.sync
            def _(e):
                P.emit("sync", e, sems)
                for sn, v in final_waits:
                    e.wait_ge(sems[sn], v)

            @block.gpsimd
            def _(e):
                P.emit("gpsimd", e, sems)

            @block.tensor
            def _(e):
                P.emit("tensor", e, sems)

            @block.scalar
            def _(e):
                P.emit("scalar", e, sems)

            @block.vector
            def _(e):
                P.emit("vector", e, sems)
    return nc, list(dbg.keys())


def _rope_tables():
    pos = np.arange(-128, SEQ + 128)
    row = (pos // 64).astype(np.float64)
    col = (pos % 64).astype(np.float64)
    inv = 10000.0 ** (-np.arange(32, dtype=np.float64) / 32.0)
    ar = row[:, None] * inv[None, :]
    ac = col[:, None] * inv[None, :]
    cr, sr, cc, sc = np.cos(ar), np.sin(ar), np.cos(ac), np.sin(ac)
    C = np.concatenate([cr, cr, cc, cc], axis=1).astype(np.float32)
    S = np.concatenate([-sr, sr, -sc, sc], axis=1).astype(np.float32)
    return C, S


_CACHE = {}


def kernel(x, c, ctx, c_ctx, w_ada, b_ada, w_in, attn_sink, gmlp_ln_g, gmlp_ln_b,
           gmlp_w_s, gmlp_b_s, w_out, ln1_g, ln1_b, w_ff1, w_ff2, ln2_g, ln2_b):
    f = lambda a: np.ascontiguousarray(np.asarray(a, dtype=np.float32))
    x, c, ctx, c_ctx = f(x), f(c), f(ctx), f(c_ctx)
    if "nc" not in _CACHE:
        _CACHE["nc"] = build_program()
    nc, dbg_names = _CACHE["nc"]

    C, S = _rope_tables()
    jj = np.arange(128)[:, None]
    ii = np.arange(128)[None, :]
    maskL = (jj >= ii).astype(np.float32)
    maskU = (jj <= ii).astype(np.float32)
    zero = np.zeros((128, 128), np.float32)
    ident = np.eye(128, dtype=np.float32)
    shared = {
        "w_ada": f(w_ada[0]), "b_ada": f(b_ada[0])[None, :],
        "b_adaT": f(np.asarray(b_ada[0]).reshape(96, 128).T),
        "w_in": f(w_in[0]), "sink": f(attn_sink[0])[None, :],
        "gln_g": f(gmlp_ln_g[0])[None, :], "gln_b": f(gmlp_ln_b[0])[None, :],
        "w_sT": f(np.asarray(gmlp_w_s[0]).transpose(2, 0, 1).reshape(128, 1024)),
        "b_s": f(np.asarray(gmlp_b_s[0]).reshape(1, 1024)),
        "w_out": f(w_out[0]),
        "ln1_g": f(ln1_g[0])[None, :], "ln1_b": f(ln1_b[0])[None, :],
        "ln1_gT": f(np.asarray(ln1_g[0]).reshape(16, 128).T), "ln1_bT": f(np.asarray(ln1_b[0]).reshape(16, 128).T),
        "w_ff1": f(w_ff1[0]), "w_ff2": f(w_ff2[0]),
        "ln2_g": f(ln2_g[0])[None, :], "ln2_b": f(ln2_b[0])[None, :],
        "ident": ident,
        "sel4": np.ascontiguousarray(np.stack([(np.arange(128) % 32 == 0), (np.arange(128) % 32 == 16)], axis=1).astype(np.float32)),
    }
    in_maps = []
    for r in range(NCORES):
        b = r // 4
        s0 = (r % 4) * TOK
        xp = np.zeros((NH * 128, D), np.float32)
        lo, hi = s0 - 128, s0 + TOK + 128
        a0, a1 = max(lo, 0), min(hi, SEQ)
        xp[a0 - lo:a1 - lo] = x[b, a0:a1]
        m = dict(shared)
        m["x_tm"] = np.ascontiguousarray(x[b, s0:s0 + TOK])
        m["xT"] = np.ascontiguousarray(xp.T)
        m["ctxT"] = np.ascontiguousarray(ctx[b].T)
        m["cvec"] = np.ascontiguousarray(np.stack([c[b], c_ctx], axis=0).reshape(2, 16, 128).transpose(2, 1, 0).reshape(128, 32))
        m["ropeC"] = np.ascontiguousarray(C[s0:s0 + NH * 128])
        m["ropeS"] = np.ascontiguousarray(S[s0:s0 + NH * 128])
        mk_ = np.stack([maskL, maskU, zero if s0 == 0 else maskL, zero if s0 + TOK == SEQ else maskU], axis=1)
        m["masks"] = np.ascontiguousarray(mk_.reshape(128, 512))
        in_maps.append(m)
    res = run_bass_kernel_spmd(nc, in_maps, core_ids=list(range(NCORES)))
    out = np.empty((2, SEQ, D), np.float32)
    for r in range(NCORES):
        b = r // 4
        s0 = (r % 4) * TOK
        out[b, s0:s0 + TOK] = res.results[r]["y"]
    if DEBUG:
        _CACHE["dbg"] = [{k: res.results[r]["dbg_" + k] for k in dbg_names} for r in range(NCORES)]
    return out
```
